# Optimizing a Trainium2 kernel written in Bass

```python
import jax, jax.numpy as jnp
from jax import lax
import numpy as np

D_MODEL = 2048
BATCH = 1
SEQ = 8192
DEPTH = 2

GRID_W = 64
CTX_LEN = 256
HEAD_DIM = 128
N_Q_HEADS = D_MODEL // HEAD_DIM
N_KV_HEADS = N_Q_HEADS // 4
GQA_GROUP = N_Q_HEADS // N_KV_HEADS
WINDOW = 128
BLOCK = 128
ROPE_AXIS_DIM = HEAD_DIM // 2
ROPE_PAIRS = ROPE_AXIS_DIM // 2
ROPE_BASE = 10000.0
POOL_WINDOWS = (2, 4, 8, 16)
POOL_WIDTH = D_MODEL // 2
POOL_GROUP = POOL_WIDTH // len(POOL_WINDOWS)
CONV_WIDTH = D_MODEL // 2
N_BRANCHES = 3
D_FF = 256 * ((8 * D_MODEL // 3 + 255) // 256)
N_MODS = 9
EPS = 1e-6
NEG_INF = -1e30

ATTN_WIDTH = N_Q_HEADS * HEAD_DIM
KV_WIDTH = N_KV_HEADS * HEAD_DIM
Q_OFF = 0
K_OFF = Q_OFF + ATTN_WIDTH
V_OFF = K_OFF + KV_WIDTH
POOL_OFF = V_OFF + KV_WIDTH
CB_OFF = POOL_OFF + POOL_WIDTH
CC_OFF = CB_OFF + CONV_WIDTH
CX_OFF = CC_OFF + CONV_WIDTH
GATE_OFF = CX_OFF + CONV_WIDTH
IN_COLS = GATE_OFF + N_BRANCHES * D_MODEL

kernel_name = "hybrid_pool_conv_swa_dit_block"


def rms_norm(x, g):
    xf = x.astype(jnp.float32)
    y = xf * lax.rsqrt(jnp.mean(xf * xf, axis=-1, keepdims=True) + EPS)
    return (y * g.astype(jnp.float32)).astype(x.dtype)


def modulate(xn, shift, scale):
    return xn * (1 + scale) + shift


def swiglu(x, wi, wo):
    g, u = jnp.split(x @ wi, 2, axis=-1)
    return (jax.nn.silu(g) * u) @ wo


def axial_rope_tables(L):
    rows = L // GRID_W
    r = jnp.repeat(jnp.arange(rows, dtype=jnp.float32), GRID_W)
    col = jnp.tile(jnp.arange(GRID_W, dtype=jnp.float32), rows)
    inv = ROPE_BASE ** (-jnp.arange(ROPE_PAIRS, dtype=jnp.float32) / ROPE_PAIRS)
    ang = jnp.stack([r[:, None] * inv, col[:, None] * inv], axis=1)
    return jnp.cos(ang), jnp.sin(ang)


def apply_axial_rope(t, cos, sin):
    B, L, H, _ = t.shape
    tt = t.reshape(B, L, H, 2, ROPE_AXIS_DIM)
    t1, t2 = tt[..., :ROPE_PAIRS], tt[..., ROPE_PAIRS:]
    c = cos[None, :, None].astype(t.dtype)
    s = sin[None, :, None].astype(t.dtype)
    out = jnp.concatenate([t1 * c - t2 * s, t2 * c + t1 * s], axis=-1)
    return out.reshape(B, L, H, HEAD_DIM)


def windowed_attention(q, k, v, kc, vc, sink):
    B, L = q.shape[0], q.shape[1]
    nb = L // BLOCK
    qb = q.reshape(B, nb, BLOCK, N_KV_HEADS, GQA_GROUP, HEAD_DIM)

    def band(t):
        tb = t.reshape(B, nb, BLOCK, N_KV_HEADS, HEAD_DIM)
        tp = jnp.pad(tb, ((0, 0), (1, 1), (0, 0), (0, 0), (0, 0)))
        return jnp.concatenate([tp[:, :-2], tp[:, 1:-1], tp[:, 2:]], axis=2)

    kb, vb = band(k), band(v)
    scale = HEAD_DIM ** -0.5
    s_loc = jnp.einsum('bnqhgd,bnshd->bnhgqs', qb, kb).astype(jnp.float32) * scale
    s_ctx = jnp.einsum('bnqhgd,bchd->bnhgqc', qb, kc).astype(jnp.float32) * scale
    blk = jnp.arange(nb)[:, None, None]
    qpos = blk * BLOCK + jnp.arange(BLOCK)[None, :, None]
    kpos = (blk - 1) * BLOCK + jnp.arange(3 * BLOCK)[None, None, :]
    valid = (jnp.abs(kpos - qpos) <= WINDOW) & (kpos >= 0) & (kpos < L)
    s_loc = jnp.where(valid[None, :, None, None], s_loc, NEG_INF)
    s_sink = jnp.broadcast_to(
        sink.astype(jnp.float32).reshape(1, 1, N_KV_HEADS, GQA_GROUP, 1, 1), s_loc.shape[:-1] + (1,))
    p = jax.nn.softmax(jnp.concatenate([s_loc, s_ctx, s_sink], axis=-1), axis=-1)
    n_loc = 3 * BLOCK
    p_loc = p[..., :n_loc].astype(v.dtype)
    p_ctx = p[..., n_loc:n_loc + kc.shape[1]].astype(v.dtype)
    o = (jnp.einsum('bnhgqs,bnshd->bnqhgd', p_loc, vb)
         + jnp.einsum('bnhgqc,bchd->bnqhgd', p_ctx, vc))
    return o.reshape(B, L, ATTN_WIDTH)


def context_attention(q, k, v, sink):
    B, Lc = q.shape[0], q.shape[1]
    qg = q.reshape(B, Lc, N_KV_HEADS, GQA_GROUP, HEAD_DIM)
    s = jnp.einsum('bqhgd,bkhd->bhgqk', qg, k).astype(jnp.float32) * (HEAD_DIM ** -0.5)
    s_sink = jnp.broadcast_to(
        sink.astype(jnp.float32).reshape(1, N_KV_HEADS, GQA_GROUP, 1, 1), s.shape[:-1] + (1,))
    p = jax.nn.softmax(jnp.concatenate([s, s_sink], axis=-1), axis=-1)[..., :-1].astype(v.dtype)
    o = jnp.einsum('bhgqk,bkhd->bqhgd', p, v)
    return o.reshape(B, Lc, ATTN_WIDTH)


def multiscale_pool(u, pool_w, pool_scale):
    B, L, _ = u.shape
    uf = u.astype(jnp.float32)
    cs = jnp.concatenate([jnp.zeros((B, 1, POOL_WIDTH), jnp.float32), jnp.cumsum(uf, axis=1)], axis=1)
    t = jnp.arange(L)
    means = []
    for gi, w in enumerate(POOL_WINDOWS):
        lo = jnp.clip(t - w // 2, 0, L - 1)
        hi = jnp.clip(t + (w - w // 2) - 1, 0, L - 1)
        seg = cs[:, :, gi * POOL_GROUP:(gi + 1) * POOL_GROUP]
        cnt = (hi - lo + 1).astype(jnp.float32)[None, :, None]
        means.append((seg[:, hi + 1] - seg[:, lo]) / cnt)
    y = (jnp.concatenate(means, axis=-1) - uf).astype(u.dtype)
    y = jnp.einsum('blgc,gcd->blgd', y.reshape(B, L, len(POOL_WINDOWS), POOL_GROUP), pool_w)
    return y.reshape(B, L, POOL_WIDTH) * pool_scale


def short_conv(v, w):
    vp = jnp.pad(v, ((0, 0), (1, 1), (0, 0)))
    return vp[:, :-2] * w[0] + vp[:, 1:-1] * w[1] + vp[:, 2:] * w[2]


def parallel_merge(p, attn_o, pool_w, pool_scale, conv_w, w_attn_out, w_pool_out, w_conv_out, w_o):
    y_attn = attn_o @ w_attn_out
    y_pool = multiscale_pool(p[..., POOL_OFF:CB_OFF], pool_w, pool_scale) @ w_pool_out
    b_gate = p[..., CB_OFF:CC_OFF]
    c_gate = p[..., CC_OFF:CX_OFF]
    xv = p[..., CX_OFF:GATE_OFF]
    y_conv = (b_gate * short_conv(c_gate * xv, conv_w)) @ w_conv_out
    g = jax.nn.sigmoid(p[..., GATE_OFF:IN_COLS].astype(jnp.float32)).astype(p.dtype)
    g = g.reshape(p.shape[:-1] + (N_BRANCHES, D_MODEL))
    merged = g[..., 0, :] * y_attn + g[..., 1, :] * y_pool + g[..., 2, :] * y_conv
    return merged @ w_o


def setup_inputs(seed: int = 0) -> dict:
    key = jax.random.key(seed)
    ks = jax.random.split(key, 24)

    def dense(k, shape, fan_in, gain=1.0):
        return jax.random.normal(k, shape, jnp.float32) * (gain * fan_in ** -0.5)

    def near_one(k, shape, s):
        return 1.0 + s * jax.random.normal(k, shape, jnp.float32)

    D, F = D_MODEL, D_FF
    return {
        "x": jax.random.normal(ks[0], (BATCH, SEQ, D), jnp.float32),
        "c": jax.random.normal(ks[1], (BATCH, D), jnp.float32),
        "ctx": jax.random.normal(ks[2], (BATCH, CTX_LEN, D), jnp.float32),
        "c_ctx": jax.random.normal(ks[3], (D,), jnp.float32),
        "w_ada": dense(ks[4], (DEPTH, D, N_MODS * D), D, 0.5),
        "b_ada": 0.01 * jax.random.normal(ks[5], (DEPTH, N_MODS * D), jnp.float32),
        "norm_g": near_one(ks[6], (DEPTH, 3, D), 0.05),
        "ffn1_wi": dense(ks[7], (DEPTH, D, 2 * F), D),
        "ffn1_wo": dense(ks[8], (DEPTH, F, D), F),
        "w_in": dense(ks[9], (DEPTH, D, IN_COLS), D),
        "attn_sink": 0.5 * jax.random.normal(ks[10], (DEPTH, N_Q_HEADS), jnp.float32),
        "pool_w": dense(ks[11], (DEPTH, len(POOL_WINDOWS), POOL_GROUP, POOL_GROUP), POOL_GROUP),
        "pool_scale": near_one(ks[12], (DEPTH, POOL_WIDTH), 0.1),
        "conv_w": dense(ks[13], (DEPTH, 3, CONV_WIDTH), 3),
        "w_attn_out": dense(ks[14], (DEPTH, ATTN_WIDTH, D), ATTN_WIDTH),
        "w_pool_out": dense(ks[15], (DEPTH, POOL_WIDTH, D), POOL_WIDTH),
        "w_conv_out": dense(ks[16], (DEPTH, CONV_WIDTH, D), CONV_WIDTH),
        "w_o": dense(ks[17], (DEPTH, D, D), D),
        "ffn2_wi": dense(ks[18], (DEPTH, D, 2 * F), D),
        "ffn2_wo": dense(ks[19], (DEPTH, F, D), F),
        "final_g": near_one(ks[20], (D,), 0.05),
    }


def reference(x, c, ctx, c_ctx, w_ada, b_ada, norm_g, ffn1_wi, ffn1_wo, w_in, attn_sink, pool_w,
              pool_scale, conv_w, w_attn_out, w_pool_out, w_conv_out, w_o, ffn2_wi, ffn2_wo, final_g):
    B, L, D = x.shape
    Lc = ctx.shape[1]
    cos, sin = axial_rope_tables(L)
    h, hc = x, ctx
    for l in range(DEPTH):
        last = l == DEPTH - 1
        m = (jax.nn.silu(c) @ w_ada[l] + b_ada[l]).reshape(B, N_MODS, 1, D)
        mc = (jax.nn.silu(c_ctx) @ w_ada[l] + b_ada[l]).reshape(N_MODS, 1, 1, D)

        h = h + 0.5 * m[:, 2] * swiglu(
            modulate(rms_norm(h, norm_g[l, 0]), m[:, 0], m[:, 1]), ffn1_wi[l], ffn1_wo[l])
        hc = hc + 0.5 * mc[2] * swiglu(
            modulate(rms_norm(hc, norm_g[l, 0]), mc[0], mc[1]), ffn1_wi[l], ffn1_wo[l])

        xn = modulate(rms_norm(h, norm_g[l, 1]), m[:, 3], m[:, 4])
        xc = modulate(rms_norm(hc, norm_g[l, 1]), mc[3], mc[4])
        if last:
            kv_c = xc @ w_in[l][:, K_OFF:POOL_OFF]
        else:
            pc = xc @ w_in[l]
            kv_c = pc[..., K_OFF:POOL_OFF]
        kc = kv_c[..., :KV_WIDTH].reshape(B, Lc, N_KV_HEADS, HEAD_DIM)
        vc = kv_c[..., KV_WIDTH:].reshape(B, Lc, N_KV_HEADS, HEAD_DIM)

        p = xn @ w_in[l]
        q = apply_axial_rope(p[..., Q_OFF:K_OFF].reshape(B, L, N_Q_HEADS, HEAD_DIM), cos, sin)
        k = apply_axial_rope(p[..., K_OFF:V_OFF].reshape(B, L, N_KV_HEADS, HEAD_DIM), cos, sin)
        v = p[..., V_OFF:POOL_OFF].reshape(B, L, N_KV_HEADS, HEAD_DIM)
        attn = windowed_attention(q, k, v, kc, vc, attn_sink[l])
        h = h + m[:, 5] * parallel_merge(p, attn, pool_w[l], pool_scale[l], conv_w[l],
                                         w_attn_out[l], w_pool_out[l], w_conv_out[l], w_o[l])
        if not last:
            qc = pc[..., Q_OFF:K_OFF].reshape(B, Lc, N_Q_HEADS, HEAD_DIM)
            attn_c = context_attention(qc, kc, vc, attn_sink[l])
            hc = hc + mc[5] * parallel_merge(pc, attn_c, pool_w[l], pool_scale[l], conv_w[l],
                                             w_attn_out[l], w_pool_out[l], w_conv_out[l], w_o[l])

        h = h + 0.5 * m[:, 8] * swiglu(
            modulate(rms_norm(h, norm_g[l, 2]), m[:, 6], m[:, 7]), ffn2_wi[l], ffn2_wo[l])
        if not last:
            hc = hc + 0.5 * mc[8] * swiglu(
                modulate(rms_norm(hc, norm_g[l, 2]), mc[6], mc[7]), ffn2_wi[l], ffn2_wo[l])
    return rms_norm(h, final_g)
```

```python
import math
from contextlib import ExitStack

import numpy as np
import ml_dtypes

import concourse.bass as bass
import concourse.mybir as mybir
from concourse.bass_utils import run_bass_kernel_spmd

F32 = mybir.dt.float32
BF16 = mybir.dt.bfloat16
AF = mybir.ActivationFunctionType
ALU = mybir.AluOpType

D = 2048
KC = 16
FF = 5632
L = 8192
NCORE = 8
OWN = 1024
CTXL = 256
NQ = 16
NKV = 4
Q_OFF = 0
K_OFF = 2048
V_OFF = 2560
POOL_OFF = 3072
CB_OFF = 4096
CC_OFF = 5120
CX_OFF = 6144
GATE_OFF = 7168
IN_COLS = 13312
POOL_WINDOWS = (2, 4, 8, 16)
EPS = 1e-6
NUNITS = 88
U_AO, U_PO, U_CO, U_WO, U_PW = 52, 60, 68, 76, 84
NLT = 1536
LAT0 = 8
NLATC = 1280
CTX0 = LAT0 + NLATC + 8
HW = CTX0 + CTXL + 8
MG = 8
TW = 256
XW = TW + 2 * MG
MASKNEG = -30000.0

ENGS = ["pe", "act", "dve", "pool", "sp"]
import os as _os
SAME_ENGINE_SYNC = not bool(_os.environ.get("K_NOSES"))


class Sched:
    def __init__(self, nc, eng_sems, dma_sems):
        self.nc = nc
        self.dma_sems = dma_sems
        self.dma_uses = {}
        self.dma_rr = {q: 0 for q in dma_sems}
        self.q = {e: [] for e in ENGS}
        self.cnt = {e: 0 for e in ENGS}
        self.lastw = {}
        self.readers = {}
        self.seen = {e: {} for e in ENGS}
        self.semobj = {}
        for e, s in eng_sems.items():
            self.semobj[("e", e)] = s
        for qn, lst in dma_sems.items():
            for i, s in enumerate(lst):
                self.semobj[("d", qn, i)] = s
                self.dma_uses[("d", qn, i)] = 0
        self.ninst = 0

    def _deps(self, eng, reads, writes):
        need = {}

        def add(k, v):
            if need.get(k, 0) < v:
                need[k] = v
        for r in reads:
            t = self.lastw.get(r)
            if t is not None:
                add(*t)
        for w in writes:
            t = self.lastw.get(w)
            if t is not None:
                add(*t)
            for k, v in self.readers.get(w, {}).items():
                add(k, v)
        waits = []
        for k, v in need.items():
            if k == ("e", eng) and not SAME_ENGINE_SYNC:
                continue
            if self.seen[eng].get(k, 0) >= v:
                continue
            self.seen[eng][k] = v
            waits.append((k, v))
        return waits

    def _commit(self, tok, reads, writes):
        k, v = tok
        for w in writes:
            self.lastw[w] = tok
            self.readers[w] = {}
        for r in reads:
            d = self.readers.setdefault(r, {})
            if d.get(k, 0) < v:
                d[k] = v

    def op(self, eng, fn, reads=(), writes=()):
        waits = self._deps(eng, reads, writes)
        self.cnt[eng] += 1
        tok = (("e", eng), self.cnt[eng])
        self.q[eng].append((waits, fn, ("e", eng), 1))
        self._commit(tok, reads, writes)
        return tok

    def dma(self, qeng, out, in_, reads=(), writes=()):
        ring = self.dma_sems[qeng]
        i = self.dma_rr[qeng]
        self.dma_rr[qeng] = (i + 1) % len(ring)
        k = ("d", qeng, i)
        prev = self.dma_uses[k] * 16
        self.dma_uses[k] += 1
        waits = self._deps(qeng, reads, writes)
        if prev > 0 and self.seen[qeng].get(k, 0) < prev:
            self.seen[qeng][k] = prev
            waits.append((k, prev))
        tok = (k, prev + 16)

        def fn(e, out=out, in_=in_):
            return e.dma_start(out=out, in_=in_)
        self.q[qeng].append((waits, fn, k, 16))
        self._commit(tok, reads, writes)
        return tok

    def wait_tokens(self, eng, toks):
        waits = []
        for k, v in toks:
            if self.seen[eng].get(k, 0) < v:
                self.seen[eng][k] = v
                waits.append((k, v))
        if waits:
            self.q[eng].append((waits, None, None, 0))

    def flush(self):
        nc = self.nc
        q = self.q
        self.q = {e: [] for e in ENGS}
        semobj = self.semobj
        for e in ENGS:
            self.ninst += len(q[e])

        def run(engobj, lst):
            for waits, fn, inck, incv in lst:
                for k, v in waits:
                    engobj.wait_ge(semobj[k], v)
                if fn is not None:
                    ins = fn(engobj)
                    ins.then_inc(semobj[inck], incv)

        with nc.Block() as block:
            if q["sp"]:
                @block.sync
                def _(e):
                    run(e, q["sp"])
            if q["pe"]:
                @block.tensor
                def _(e):
                    run(e, q["pe"])
            if q["act"]:
                @block.scalar
                def _(e):
                    run(e, q["act"])
            if q["dve"]:
                @block.vector
                def _(e):
                    run(e, q["dve"])
            if q["pool"]:
                @block.gpsimd
                def _(e):
                    run(e, q["pool"])


class Ring:
    def __init__(self, tiles, name):
        self.tiles = tiles
        self.name = name
        self.i = 0

    def get(self):
        t = self.tiles[self.i]
        k = (self.name, self.i)
        self.i = (self.i + 1) % len(self.tiles)
        return t, k


def hres(c, a, b):
    return [("h", c, blk) for blk in range(a // 128, (b - 1) // 128 + 1)]


class Builder:
    def __init__(self, layers, first, last, stop=None):
        self.stop = stop
        self.layers = layers
        self.first = first
        self.last = last
        self.nc = bass.Bass("TRN2", target_bir_lowering=False)
        self.dram = {}

    def sbt(self, name, shape, dt=F32):
        self._uid = getattr(self, "_uid", 0) + 1
        return self.nc.sbuf_tensor(f"{name}_u{self._uid}", list(shape), dt)

    def din(self, name, shape, dt=F32):
        t = self.nc.dram_tensor(name, list(shape), dt, kind="ExternalInput")
        self.dram[name] = t
        return t

    def build(self):
        nc = self.nc
        dr = self.dram
        if self.first:
            self.din("xin", [128, KC, NLT])
            self.din("ctxin", [128, KC, CTXL])
        else:
            self.din("hin", [128, KC, HW])
        self.din("ccT", [128, KC, 2])
        self.din("bada", [128, 2, 144])
        self.din("ng", [128, 2, 3, KC])
        self.din("fg", [128, KC])
        self.din("sinkb", [128, 2, NQ])
        self.din("pscale", [128, 2, 8])
        self.din("convw", [128, 2, 3, 8])
        self.din("ropeC", [128, NLT])
        self.din("ropeS", [128, NLT])
        self.din("valid", [128, NLT])
        self.din("invcnt", [128, 4, NLT])
        self.din("invcntc", [128, 4, CTXL])
        self.din("maskb", [128, 12, 2, 128], BF16)
        self.din("ident", [128, 128], BF16)
        for l in self.layers:
            self.din(f"w_ada{l}", [D, 9 * D])
            self.din(f"ffn1_wi{l}", [D, 2 * FF])
            self.din(f"ffn1_wo{l}", [FF, D])
            self.din(f"w_in{l}", [D, IN_COLS])
            self.din(f"pool_w{l}", [4, 256, 256])
            self.din(f"w_attn_out{l}", [D, D])
            self.din(f"w_pool_out{l}", [1024, D])
            self.din(f"w_conv_out{l}", [1024, D])
            self.din(f"w_o{l}", [D, D])
            self.din(f"ffn2_wi{l}", [D, 2 * FF])
            self.din(f"ffn2_wo{l}", [FF, D])
        self.cache = {l: nc.dram_tensor(f"wcache{l}", [NUNITS, 128, KC * 256], BF16, kind="Internal") for l in self.layers}
        self.jobs = {l: self.make_jobs(l) for l in self.layers}
        if self.last:
            self.out = nc.dram_tensor("out", [128, KC, OWN], F32, kind="ExternalOutput")
        else:
            self.out = nc.dram_tensor("hout", [128, KC, HW], F32, kind="ExternalOutput")

        with ExitStack() as es:
            E = es.enter_context
            sb = lambda n, s, d=F32: E(nc.sbuf_tensor(n, list(s), d))
            self.h = sb("h_main", [128, KC, HW])
            self.Ko = sb("Ko", [128, NKV, 256], BF16)
            self.Vo = sb("Vo", [128, 2, 512], BF16)
            self.ones32 = sb("ones32", [128, 128])
            self.onesb = sb("onesb", [128, 128], BF16)
            self.identb = sb("identb", [128, 128], BF16)
            self.epst = sb("epst", [128, 1])
            self.mT = sb("mT", [128, 144, 2])
            self.Av = sb("Av", [128, 3, 2, KC])
            self.Bv = sb("Bv", [128, 3, 2, KC])
            self.Gv = sb("Gv", [128, 3, 2, KC])
            self.ngs = sb("ngs", [128, 2, 3, KC])
            self.fgs = sb("fgs", [128, KC])
            self.badas = sb("badas", [128, 2, 144])
            self.sinks = sb("sinks", [128, 2, NQ])
            self.sinkexp = sb("sinkexp", [128, NQ])
            self.pscales = sb("pscales", [128, 2, 8])
            self.convws = sb("convws", [128, 2, 3, 8])
            self.cc32 = sb("cc32", [128, KC, 2])
            self.scb = sb("scb", [128, KC, 2], BF16)
            self.psum = [E(nc.psum_tensor(f"ps{i}", [128, 512], F32)) for i in range(8)]
            self.PS = Ring(self.psum, "ps")
            eng_sems = {e: E(nc.semaphore("s_" + e)) for e in ENGS}
            dma_sems = {"sp": [E(nc.semaphore(f"d_sp{i}")) for i in range(8)],
                        "pool": [E(nc.semaphore(f"d_pool{i}")) for i in range(8)]}
            self.S = Sched(nc, eng_sems, dma_sems)
            self.out_toks = []

            self.emit_init()
            for l in self.layers:
                self.emit_ada(l)
                latg = [[(LAT0, 512, 0), (LAT0 + 512, 384, 0)], [(LAT0 + 896, 384, 0), (CTX0, 256, 1)]]
                if l == 0:
                    self.emit_pre()
                    self.emit_ffn(l, 1, self.h, latg, hres, jobs=(0, 2))
                    if self.stop == "ffn1":
                        break
                    kvt = [(LAT0 + 256 * i, 0, i * 2) for i in range(5)] + [(CTX0, 1, 10)]
                    qt = [(LAT0 + 256 * i, 0) for i in range(5)] + [(CTX0, 1)]
                    self.emit_mixer(l, kvt, qt)
                    if self.stop == "mixer":
                        break
                    self.emit_ffn(l, 2, self.h, latg, hres, jobs=(1, 2))
                else:
                    self.emit_ffn(l, 1, self.h, latg, hres)
                    kvt = [(LAT0 + 256 * i, 0, i * 2) for i in range(5)] + [(CTX0, 1, 10)]
                    qt = [(LAT0 + 128 + 256 * i, 0) for i in range(4)]
                    self.emit_mixer(l, kvt, qt)
                    o0 = LAT0 + 128
                    self.emit_ffn(l, 2, self.h, [[(o0, 512, 0)], [(o0 + 512, 512, 0)]], hres)
            self.emit_final()
        return nc

    def cunit(self, l, u, nk=KC):
        return self.cache[l][u, :, 0:nk * 256].rearrange("p (k c) -> p k c", c=256)

    def make_jobs(self, l):
        dr = self.dram
        jobs = []
        order = list(range(8, 12)) + list(range(0, 8)) + list(range(12, 52))
        for u in order:
            jobs.append((self.cunit(l, u), dr[f"w_in{l}"][:, u * 256:(u + 1) * 256].rearrange("(k p) c -> p k c", p=128), ("wc", l, u)))
        for gi in range(4):
            jobs.append((self.cunit(l, U_PW + gi, 2), dr[f"pool_w{l}"][gi, :, :].rearrange("(i p) c -> p i c", p=128), ("wc", l, U_PW + gi)))
        for base, nm, nk in ((U_AO, "w_attn_out", KC), (U_PO, "w_pool_out", 8), (U_CO, "w_conv_out", 8), (U_WO, "w_o", KC)):
            for dp in range(8):
                jobs.append((self.cunit(l, base + dp, nk),
                             dr[f"{nm}{l}"][:, dp * 256:(dp + 1) * 256].rearrange("(k p) c -> p k c", p=128), ("wc", l, base + dp)))
        return jobs

    def issue_jobs(self, l, n):
        jl = self.jobs.get(l)
        while jl and n > 0:
            dst, src, key = jl.pop(0)
            self.S.dma("pool", dst, src, writes=[key])
            n -= 1

    def emit_init(self):
        S, nc, dr, h = self.S, self.nc, self.dram, self.h
        allh = [("h", c, b) for c in range(KC) for b in range((HW + 127) // 128)]
        S.op("pool", lambda e: e.memset(self.ones32[:], 1.0), writes=["ones32"])
        S.op("pool", lambda e: e.memset(self.onesb[:], 1.0), writes=["onesb"])
        S.op("pool", lambda e: e.memset(self.epst[:], EPS), writes=["eps"])
        S.op("pool", lambda e: e.memset(h[:, :, 0:LAT0], 0.0), writes=allh)
        S.op("pool", lambda e: e.memset(h[:, :, LAT0 + NLATC:CTX0], 0.0), writes=allh)
        S.op("pool", lambda e: e.memset(h[:, :, CTX0 + CTXL:HW], 0.0), writes=allh)
        if self.first:
            for c0 in range(0, KC, 4):
                S.dma("sp", h[:, c0:c0 + 4, LAT0:LAT0 + NLATC], dr["xin"][:, c0:c0 + 4, 128:128 + NLATC], writes=allh)
            S.dma("sp", h[:, :, CTX0:CTX0 + CTXL], dr["ctxin"][:, :, :], writes=allh)
        else:
            for c0 in range(0, KC, 4):
                S.dma("sp", h[:, c0:c0 + 4, :], dr["hin"][:, c0:c0 + 4, :], writes=allh)
        S.dma("sp", self.identb[:], dr["ident"][:, :], writes=["identb"])
        S.dma("sp", self.ngs[:], dr["ng"][:, :, :, :], writes=["ngs"])
        S.dma("sp", self.fgs[:], dr["fg"][:, :], writes=["fgs"])
        S.dma("sp", self.badas[:], dr["bada"][:, :, :], writes=["badas"])
        S.dma("sp", self.sinks[:], dr["sinkb"][:, :, :], writes=["sinks"])
        S.dma("sp", self.pscales[:], dr["pscale"][:, :, :], writes=["pscales"])
        S.dma("sp", self.convws[:], dr["convw"][:, :, :, :], writes=["convws"])
        S.dma("sp", self.cc32[:], dr["ccT"][:, :, :], writes=["cc32"])
        S.op("act", lambda e: e.activation(out=self.scb[:], in_=self.cc32[:], func=AF.Silu),
             reads=["cc32"], writes=["scb"])
        S.flush()

    def emit_ada(self, l):
        S, nc, dr = self.S, self.nc, self.dram
        wada = dr[f"w_ada{l}"]
        with ExitStack() as es:
            wa = [es.enter_context(self.sbt(f"wa{l}_{i}", [128, KC, 512], BF16)) for i in range(3)]
            WA = Ring(wa, "wa")
            for piece in range(36):
                w, wk = WA.get()
                S.dma("pool", w[:], wada[:, piece * 512:(piece + 1) * 512].rearrange("(k p) c -> p k c", p=128),
                      writes=[wk])
                ps, pk = self.PS.get()

                def mm(e, w=w, ps=ps):
                    ins = None
                    for j in range(4):
                        for k in range(KC):
                            ins = e.matmul(ps[:, 2 * j:2 * j + 2], w[:, k, j * 128:(j + 1) * 128], self.scb[:, k, :],
                                           start=(k == 0), stop=(k == KC - 1))
                    return ins
                S.op("pe", mm, reads=[wk, "scb"], writes=[pk])
                psv = ps[:, 0:8].rearrange("p (j v) -> p j v", v=2)
                bb = self.badas[:, l, piece * 4:(piece + 1) * 4].unsqueeze(2).to_broadcast([128, 4, 2])
                S.op("dve", lambda e, psv=psv, bb=bb, piece=piece: e.tensor_tensor(
                    out=self.mT[:, piece * 4:(piece + 1) * 4, :], in0=psv, in1=bb, op=ALU.add),
                    reads=[pk, "badas"], writes=["mT"])
            for n in range(3):
                for v in range(2):
                    sh = self.mT[:, 16 * (3 * n):16 * (3 * n) + 16, v]
                    sc = self.mT[:, 16 * (3 * n + 1):16 * (3 * n + 1) + 16, v]
                    gt = self.mT[:, 16 * (3 * n + 2):16 * (3 * n + 2) + 16, v]
                    S.op("dve", lambda e, n=n, v=v, sc=sc: e.scalar_tensor_tensor(
                        out=self.Av[:, n, v, :], in0=sc, scalar=1.0, in1=self.ngs[:, l, n, :],
                        op0=ALU.add, op1=ALU.mult), reads=["mT", "ngs"], writes=["Av"])
                    S.op("dve", lambda e, n=n, v=v, sh=sh: e.tensor_copy(out=self.Bv[:, n, v, :], in_=sh),
                         reads=["mT"], writes=["Bv"])
                    gm = 1.0 if n == 1 else 0.5
                    S.op("dve", lambda e, n=n, v=v, gt=gt, gm=gm: e.tensor_scalar_mul(
                        out=self.Gv[:, n, v, :], in0=gt, scalar1=gm), reads=["mT"], writes=["Gv"])
            S.op("act", lambda e: e.activation(out=self.sinkexp[:], in_=self.sinks[:, l, :], func=AF.Exp),
                 reads=["sinks"], writes=["sinkexp"])
            S.flush()

    def norm_mod(self, hbuf, hresf, col0, n, nidx, v, dst, dstkey, tmp, sqr, rstd, rstdk, valid=None, validk=None):
        S = self.S
        ps, pk = self.PS.get()
        for c in range(KC):
            sq, sk = sqr.get()
            S.op("act", lambda e, sq=sq, c=c: e.activation(out=sq[:, 0:n], in_=hbuf[:, c, col0:col0 + n], func=AF.Square),
                 reads=hresf(c, col0, col0 + n), writes=[sk])
            S.op("pe", lambda e, sq=sq, c=c, ps=ps: e.matmul(ps[:, 0:n], self.ones32[:], sq[:, 0:n],
                                                          start=(c == 0), stop=(c == KC - 1)),
                 reads=[sk, "ones32"], writes=[pk])
        S.op("act", lambda e, ps=ps: e.activation(out=rstd[:, 0:n], in_=ps[:, 0:n], func=AF.Sqrt,
                                                  bias=self.epst[:, 0:1], scale=1.0 / D),
             reads=[pk, "eps"], writes=[rstdk])
        S.op("dve", lambda e: e.reciprocal(out=rstd[:, 0:n], in_=rstd[:, 0:n]), reads=[rstdk], writes=[rstdk])
        if valid is not None:
            S.op("dve", lambda e: e.tensor_tensor(out=rstd[:, 0:n], in0=rstd[:, 0:n], in1=valid, op=ALU.mult),
                 reads=[rstdk, validk], writes=[rstdk])
        for c in range(KC):
            t, tk = tmp.get()
            S.op("dve", lambda e, t=t, c=c: e.scalar_tensor_tensor(
                out=t[:, 0:n], in0=hbuf[:, c, col0:col0 + n], scalar=self.Av[:, nidx, v, c:c + 1], in1=rstd[:, 0:n],
                op0=ALU.mult, op1=ALU.mult), reads=hresf(c, col0, col0 + n) + [rstdk, "Av"], writes=[tk])
            if valid is None:
                S.op("act", lambda e, t=t, c=c: e.activation(out=dst(c), in_=t[:, 0:n], func=AF.Identity,
                                                            bias=self.Bv[:, nidx, v, c:c + 1], scale=1.0),
                     reads=[tk, "Bv"], writes=[dstkey])
            else:
                S.op("dve", lambda e, t=t, c=c: e.scalar_tensor_tensor(
                    out=dst(c), in0=valid, scalar=self.Bv[:, nidx, v, c:c + 1], in1=t[:, 0:n],
                    op0=ALU.mult, op1=ALU.add), reads=[tk, "Bv", validk], writes=[dstkey])

    def emit_ffn(self, l, which, hbuf, groups, hresf, jobs=None):
        S, nc, dr = self.S, self.nc, self.dram
        nidx = 0 if which == 1 else 2
        wi = dr[f"ffn{which}_wi{l}"]
        wo = dr[f"ffn{which}_wo{l}"]
        XNW = max(sum(t[1] for t in g) for g in groups)
        with ExitStack() as es:
            E = es.enter_context
            xn = E(self.sbt("f_xn", [128, KC, XNW], BF16))
            wig = [E(self.sbt(f"f_wig{i}", [128, KC, 256], BF16)) for i in range(2)]
            wiu = [E(self.sbt(f"f_wiu{i}", [128, KC, 256], BF16)) for i in range(2)]
            wos = [E(self.sbt(f"f_wo{i}", [128, 2, D], BF16)) for i in range(2)]
            h1 = Ring([E(self.sbt(f"f_h1{i}", [128, 512], BF16)) for i in range(4)], "f_h1")
            sg = Ring([E(self.sbt(f"f_sg{i}", [128, 512], F32)) for i in range(2)], "f_sg")
            tmp = Ring([E(self.sbt(f"f_tmp{i}", [128, 512], F32)) for i in range(2)], "f_tmp")
            sqr = Ring([E(self.sbt(f"f_sq{i}", [128, 512], F32)) for i in range(2)], "f_sq")
            rstds = [E(self.sbt(f"f_rstd{i}", [128, 512], F32)) for i in range(2)]
            ri = 0
            for gi, grp in enumerate(groups):
                offs = []
                o = 0
                for ti, (col0, n, v) in enumerate(grp):
                    offs.append(o)
                    rstd = rstds[ri % 2]
                    rk = ("f_rstd", ri % 2)
                    ri += 1
                    self.norm_mod(hbuf, hresf, col0, n, nidx, v,
                                  (lambda c, o=o, n=n: xn[:, c, o:o + n]), ("f_xn", ti), tmp, sqr, rstd, rk)
                    o += n
                for s in range(FF // 256):
                    sl = s % 2
                    S.dma("pool", wig[sl][:], wi[:, s * 256:(s + 1) * 256].rearrange("(k p) c -> p k c", p=128),
                          writes=[("f_wig", sl)])
                    S.dma("pool", wiu[sl][:], wi[:, FF + s * 256:FF + (s + 1) * 256].rearrange("(k p) c -> p k c", p=128),
                          writes=[("f_wiu", sl)])
                    S.dma("pool", wos[sl][:], wo[s * 256:(s + 1) * 256, :].rearrange("(j p) c -> p j c", p=128),
                          writes=[("f_wo", sl)])
                    if jobs is not None:
                        self.issue_jobs(jobs[0], jobs[1])
                    for ti, (col0, n, v) in enumerate(grp):
                        o = offs[ti]
                        hs = []
                        for j in range(2):
                            pg, pgk = self.PS.get()
                            pu, puk = self.PS.get()

                            def mmg(e, w=wig[sl], ps=pg, j=j, o=o, n=n):
                                ins = None
                                for k in range(KC):
                                    ins = e.matmul(ps[:, 0:n], w[:, k, j * 128:(j + 1) * 128], xn[:, k, o:o + n],
                                                   start=(k == 0), stop=(k == KC - 1))
                                return ins
                            S.op("pe", mmg, reads=[("f_wig", sl), ("f_xn", ti)], writes=[pgk])
                            S.op("pe", lambda e, w=wiu[sl], ps=pu, j=j, o=o, n=n: mmg(e, w, ps, j, o, n),
                                 reads=[("f_wiu", sl), ("f_xn", ti)], writes=[puk])
                            sgt, sgk = sg.get()
                            S.op("act", lambda e, sgt=sgt, pg=pg, n=n: e.activation(out=sgt[:, 0:n], in_=pg[:, 0:n], func=AF.Silu),
                                 reads=[pgk], writes=[sgk])
                            ht, hk = h1.get()
                            S.op("dve", lambda e, ht=ht, sgt=sgt, pu=pu, n=n: e.tensor_tensor(
                                out=ht[:, 0:n], in0=pu[:, 0:n], in1=sgt[:, 0:n], op=ALU.mult),
                                reads=[puk, sgk], writes=[hk])
                            hs.append((ht, hk))
                        for d in range(KC):
                            po, pok = self.PS.get()

                            def mmo(e, w=wos[sl], po=po, d=d, n=n, hs=hs):
                                ins = None
                                for j in range(2):
                                    ins = e.matmul(po[:, 0:n], w[:, j, d * 128:(d + 1) * 128], hs[j][0][:, 0:n],
                                                   start=(j == 0), stop=(j == 1))
                                return ins
                            S.op("pe", mmo, reads=[("f_wo", sl), hs[0][1], hs[1][1]], writes=[pok])
                            hr = hresf(d, col0, col0 + n)
                            S.op("dve", lambda e, po=po, d=d, n=n, col0=col0, v=v: e.scalar_tensor_tensor(
                                out=hbuf[:, d, col0:col0 + n], in0=po[:, 0:n], scalar=self.Gv[:, nidx, v, d:d + 1],
                                in1=hbuf[:, d, col0:col0 + n], op0=ALU.mult, op1=ALU.add),
                                reads=[pok, "Gv"] + hr, writes=hr)
                S.flush()

    def emit_pre(self):
        S, nc, dr = self.S, self.nc, self.dram
        with ExitStack() as es:
            ht = es.enter_context(self.sbt("h_tmp", [128, KC, 256], F32))
            hr = lambda c, a, b: [("ht", c)]
            allht = [("ht", c) for c in range(KC)]
            S.dma("sp", ht[:, :, 0:128], dr["xin"][:, :, 0:128], writes=allht)
            S.dma("sp", ht[:, :, 128:256], dr["xin"][:, :, NLT - 128:NLT], writes=allht)
            self.emit_ffn(0, 1, ht, [[(0, 256, 0)]], hr, jobs=(0, 1))
            for c0 in range(0, KC, 8):
                S.op("act", lambda e, c0=c0: e.copy(out=self.h[:, c0:c0 + 8, 0:LAT0], in_=ht[:, c0:c0 + 8, 128 - MG:128]),
                     reads=allht, writes=[("h", c, 0) for c in range(c0, c0 + 8)])
                S.op("act", lambda e, c0=c0: e.copy(out=self.h[:, c0:c0 + 8, LAT0 + NLATC:CTX0],
                                                    in_=ht[:, c0:c0 + 8, 128:128 + MG]),
                     reads=allht, writes=[("h", c, (LAT0 + NLATC) // 128) for c in range(c0, c0 + 8)])
            self.emit_mixer(0, [], [], pre=ht, pre_res=hr)

    def emit_mixer(self, l, kvtiles, qtiles, pre=None, pre_res=None):
        S, nc, dr = self.S, self.nc, self.dram
        w_in = dr[f"w_in{l}"]
        last_layer = (l == 1)
        scale = 1.0 / math.sqrt(128.0)
        with ExitStack() as es:
            E = es.enter_context
            sb = lambda n, s, d=F32: E(self.sbt(n, list(s), d))
            xn = sb("m_xn", [128, KC, XW], BF16)
            swb = [sb(f"m_sw{i}", [128, TW]) for i in range(1)]
            WU = Ring([sb(f"m_wu{i}", [128, KC, 256], BF16) for i in range(3)], "m_wu")
            tmp = Ring([sb(f"m_tmp{i}", [128, XW]) for i in range(6)], "m_tmp")
            sqr = tmp
            ropeCS = sb("m_ropeCS", [128, 2 * TW])
            ropeC = ropeCS[:, 0:TW]
            ropeS = ropeCS[:, TW:2 * TW]
            rstd = ropeCS
            validt = sb("m_valid", [128, XW])
            if pre is None:
                Kt = sb("m_K", [128, NKV, 12 * 128], BF16)
                Vt = sb("m_V", [128, 12, 512], BF16)
                bufQ = sb("m_q", [128, NQ, TW], BF16)
                oT = sb("m_o", [128, NQ, TW], BF16)
                ypool = sb("m_ypool", [128, 8, TW], BF16)
                poolmix = sb("m_poolmix", [128, 8, TW], BF16)
                yconv = ypool
                PT = sb("m_PT", [128, 5, 512], BF16)
                maskt = sb("m_mask", [128, 2, 2, 128], BF16)
                invc = sb("m_invc", [128, TW])
                snap = sb("m_snap", [128, KC, 4, MG])

            if pre is None:
                self.issue_jobs(l, 10 ** 6)

            def load_unit(u, nk=KC):
                w, wk = WU.get()
                S.dma("sp", w[:, 0:nk, :], self.cunit(l, u, nk), reads=[("wc", l, u)], writes=[wk])
                return w, wk

            def win_unit(c0):
                return load_unit(c0 // 256)

            def proj(w, wk, off, a, n, extra_reads=(), xb=None, xk="m_xn"):
                ps, pk = self.PS.get()
                xb = xn if xb is None else xb

                def mm(e):
                    ins = None
                    for k in range(KC):
                        ins = e.matmul(ps[:, 0:n], w[:, k, off:off + 128], xb[:, k, a:a + n],
                                       start=(k == 0), stop=(k == KC - 1))
                    return ins
                S.op("pe", mm, reads=[wk, xk] + list(extra_reads), writes=[pk])
                return ps, pk

            swi = [0]

            def rope_out(pt, ptk, n, dst, dstkey):
                i = swi[0] % len(swb)
                swi[0] += 1
                sw = swb[i]
                keys = [("m_sw", i, q) for q in range(4)]
                for q, (eng, dp, sp_) in enumerate((("act", 0, 32), ("act", 32, 0), ("dve", 64, 96), ("dve", 96, 64))):
                    if eng == "act":
                        S.op("act", lambda e, dp=dp, sp_=sp_: e.copy(out=sw[dp:dp + 32, 0:n], in_=pt[sp_:sp_ + 32, 0:n]),
                             reads=[ptk], writes=[keys[q]])
                    else:
                        S.op("dve", lambda e, dp=dp, sp_=sp_: e.tensor_copy(out=sw[dp:dp + 32, 0:n], in_=pt[sp_:sp_ + 32, 0:n]),
                             reads=[ptk], writes=[keys[q]])
                t1, t1k = tmp.get()
                S.op("dve", lambda e: e.tensor_tensor(out=t1[:, 0:n], in0=pt[:, 0:n], in1=ropeCS[:, 0:n], op=ALU.mult),
                     reads=[ptk, "m_rope"], writes=[t1k])
                S.op("pool", lambda e: e.tensor_tensor(out=sw[:, 0:n], in0=sw[:, 0:n], in1=ropeCS[:, TW:TW + n], op=ALU.mult),
                     reads=keys + ["m_rope"], writes=keys)
                S.op("pool", lambda e: e.tensor_tensor(out=dst, in0=t1[:, 0:n], in1=sw[:, 0:n], op=ALU.add),
                     reads=[t1k] + keys, writes=[dstkey])

            def do_norm(hbuf, hresf, e0, n, v, lt0, segs=None, xb=None, xk="m_xn"):
                if segs is None:
                    segs = [(hbuf, hresf, e0, n, 0)]
                xb = xn if xb is None else xb
                if v == 0:
                    S.dma("sp", validt[:, 0:n], dr["valid"][:, lt0:lt0 + n], writes=["m_valid"])
                for (sbuf_, sres, scol, sn, soff) in segs:
                    if v == 0:
                        self.norm_mod(sbuf_, sres, scol, sn, 1, v, (lambda c, soff=soff, sn=sn: xb[:, c, soff:soff + sn]),
                                      xk, tmp, sqr, rstd, "m_rope",
                                      valid=validt[:, soff:soff + sn], validk="m_valid")
                    else:
                        self.norm_mod(sbuf_, sres, scol, sn, 1, v, (lambda c, soff=soff, sn=sn: xb[:, c, soff:soff + sn]),
                                      xk, tmp, sqr, rstd, "m_rope")

            def kv_tile(hbuf, hresf, col0, n, v, lt0, Kdst, Kkey, Vdst, Vkey, xb=None, xk="m_xn"):
                xb = xn if xb is None else xb
                do_norm(hbuf, hresf, col0, n, v, lt0, xb=xb, xk=xk)
                if v == 0:
                    S.dma("sp", ropeCS[:, 0:n], dr["ropeC"][:, lt0:lt0 + n], writes=["m_rope"])
                    S.dma("sp", ropeCS[:, TW:TW + n], dr["ropeS"][:, lt0:lt0 + n], writes=["m_rope"])
                for u in range(2):
                    w, wk = win_unit(K_OFF + u * 256)
                    for hh in range(2):
                        g = 2 * u + hh
                        pt, ptk = proj(w, wk, hh * 128, 0, n, xb=xb, xk=xk)
                        if v == 0:
                            rope_out(pt, ptk, n, Kdst(g), Kkey)
                        else:
                            S.op("act", lambda e, pt=pt, g=g: e.copy(out=Kdst(g), in_=pt[:, 0:n]), reads=[ptk], writes=[Kkey])
                for u in range(2):
                    w, wk = win_unit(V_OFF + u * 256)
                    for hh in range(2):
                        g = 2 * u + hh
                        for b in range(n // 128):
                            ps, pk = self.PS.get()

                            def mmv(e, ps=ps, w=w, hh=hh, b=b):
                                ins = None
                                for k in range(KC):
                                    ins = e.matmul(ps[:, 0:128], xb[:, k, b * 128:(b + 1) * 128],
                                                   w[:, k, hh * 128:(hh + 1) * 128], start=(k == 0), stop=(k == KC - 1))
                                return ins
                            S.op("pe", mmv, reads=[wk, xk], writes=[pk])
                            S.op("act", lambda e, ps=ps, b=b, g=g: e.copy(out=Vdst(b, g), in_=ps[:, 0:128]),
                                 reads=[pk], writes=[Vkey])

            if pre is not None:
                for bi, lt0 in enumerate((0, NLT - 128)):
                    kv_tile(pre, pre_res, bi * 128, 128, 0, lt0,
                            (lambda g, bi=bi: self.Ko[:, g, bi * 128:(bi + 1) * 128]), "Ko",
                            (lambda b, g, bi=bi: self.Vo[:, bi, g * 128:(g + 1) * 128]), "Vo")
                S.flush()
                return

            allq0 = [("m_q", g) for g in range(NKV)]
            for ti0, (col0, v, kb0) in enumerate(kvtiles):
                lt0 = col0 + 120
                alt = (ti0 % 2 == 1)
                kv_tile(self.h, hres, col0, TW, v, lt0,
                        (lambda g, kb0=kb0: Kt[:, g, kb0 * 128:kb0 * 128 + TW]), ("m_K", kb0 // 2),
                        (lambda b, g, kb0=kb0: Vt[:, kb0 + b, g * 128:(g + 1) * 128]), ("m_V", kb0 // 2),
                        xb=(bufQ if alt else None), xk=("m_xnb" if alt else "m_xn"))
            if kvtiles:
                S.op("dve", lambda e: e.memset(bufQ[:, 0, 0:2], 0.0), reads=["m_xnb"], writes=allq0 + ["m_xnb"])
            allK = [("m_K", i) for i in range(6)] + ["Ko"]
            allV = [("m_V", i) for i in range(6)] + ["Vo"]

            def Kblk(g, ltb):
                if ltb == 0:
                    return self.Ko[:, g, 0:128]
                if ltb == 11:
                    return self.Ko[:, g, 128:256]
                i = ltb - 1 if ltb < 12 else ltb - 2
                return Kt[:, g, i * 128:(i + 1) * 128]

            def Vblk(g, ltb):
                if ltb == 0:
                    return self.Vo[:, 0, g * 128:(g + 1) * 128]
                if ltb == 11:
                    return self.Vo[:, 1, g * 128:(g + 1) * 128]
                i = ltb - 1 if ltb < 12 else ltb - 2
                return Vt[:, i, g * 128:(g + 1) * 128]


            snapres = lambda c, a, b: [("m_snap", a // MG)]
            need_snap = {}
            for ti, (col0, v) in enumerate(qtiles):
                if v == 0 and any(c2 + TW == col0 for (c2, v2) in qtiles[:ti]):
                    si = len(need_snap)
                    need_snap[ti] = si
                    S.op("act", lambda e, si=si, col0=col0: e.copy(out=snap[:, :, si, :], in_=self.h[:, :, col0 - MG:col0]),
                         reads=[("h", c, (col0 - MG) // 128) for c in range(KC)], writes=[("m_snap", si)])
            snapf = snap[:, :, :, :].rearrange("p k t m -> p k (t m)")
            for ti, (col0, v) in enumerate(qtiles):
                e0 = col0 - MG
                lt0 = col0 + 120
                if ti in need_snap:
                    do_norm(self.h, hres, e0, XW, v, lt0 - MG,
                            segs=[(snapf, snapres, need_snap[ti] * MG, MG, 0), (self.h, hres, col0, XW - MG, MG)])
                else:
                    do_norm(self.h, hres, e0, XW, v, lt0 - MG)
                if v == 1:
                    S.op("dve", lambda e: e.memset(xn[:, :, 0:MG], 0.0), reads=[], writes=["m_xn"])
                    S.op("dve", lambda e: e.memset(xn[:, :, MG + TW:XW], 0.0), reads=[], writes=["m_xn"])
                else:
                    S.dma("sp", ropeCS[:, 0:TW], dr["ropeC"][:, lt0:lt0 + TW], writes=["m_rope"])
                    S.dma("sp", ropeCS[:, TW:2 * TW], dr["ropeS"][:, lt0:lt0 + TW], writes=["m_rope"])
                    jb0 = lt0 // 128
                    S.dma("sp", maskt[:], dr["maskb"][:, jb0:jb0 + 2, :, :], writes=["m_mask"])
                def q_group(g):
                    for u in range(2):
                        w, wk = win_unit(Q_OFF + (4 * g + 2 * u) * 128)
                        for hh in range(2):
                            head = 4 * g + 2 * u + hh
                            pt, ptk = proj(w, wk, hh * 128, MG, TW)
                            if v == 0:
                                rope_out(pt, ptk, TW, bufQ[:, head, :], ("m_q", g))
                            else:
                                S.op("act", lambda e, pt=pt, head=head: e.copy(out=bufQ[:, head, :], in_=pt[:, 0:TW]),
                                     reads=[ptk], writes=[("m_q", g)])

                def att_scores(g, b):
                        if v == 0:
                            jb = lt0 // 128 + b
                            keys = [(jb - 1, 0), (jb, None), (jb + 1, 1), (12, None), (13, None)]
                        else:
                            keys = [(12, None), (13, None)]
                        qv = bufQ[:, 4 * g:4 * g + 4, b * 128:(b + 1) * 128]
                        nk = len(keys)
                        for ki, (kb, side) in enumerate(keys):
                            ps, pk = self.PS.get()

                            def mms(e, ps=ps, kb=kb, side=side, b=b, qv=qv, g=g):
                                pv = ps[:, :].rearrange("p (a q) -> p a q", a=4)
                                ins = e.matmul(pv, Kblk(g, kb), qv, start=True, stop=(side is None))
                                if side is not None:
                                    mb = maskt[:, b, side, :].unsqueeze(1).to_broadcast([128, 4, 128])
                                    ins = e.matmul(pv, self.identb[:], mb, start=False, stop=True)
                                return ins
                            S.op("pe", mms, reads=allK + [("m_q", g), "m_mask", "identb"], writes=[pk])
                            S.op("act", lambda e, ps=ps, ki=ki: e.activation(out=PT[:, ki, :], in_=ps[:, :], func=AF.Exp, scale=scale),
                                 reads=[pk], writes=[("m_PT", ki)])
                        return keys, nk

                def att_pv(g, b, keys, nk):
                        po, pok = self.PS.get()
                        pd, pdk = self.PS.get()

                        def mmpv(e, po=po, keys=keys, g=g):
                            ins = None
                            for ki, (kb, side) in enumerate(keys):
                                ins = e.matmul(po[:, :], Vblk(g, kb), PT[:, ki, :], start=(ki == 0), stop=(ki == len(keys) - 1))
                            return ins

                        def mmden(e, pd=pd, keys=keys):
                            ins = None
                            for ki in range(len(keys)):
                                ins = e.matmul(pd[:, :], self.onesb[:], PT[:, ki, :], start=(ki == 0), stop=(ki == len(keys) - 1))
                            return ins
                        ptr = [("m_PT", ki) for ki in range(nk)]
                        S.op("pe", mmpv, reads=allV + ptr, writes=[pok])
                        S.op("pe", mmden, reads=["onesb"] + ptr, writes=[pdk])
                        for hf in range(2):
                            dt_, dk_ = tmp.get()
                            h0 = 4 * g + 2 * hf
                            sx = self.sinkexp[:, h0:h0 + 2].unsqueeze(2).to_broadcast([128, 2, 128])
                            dv = dt_[:, 0:256].rearrange("p (a q) -> p a q", a=2)
                            S.op("dve", lambda e, pd=pd, sx=sx, dv=dv, hf=hf: e.tensor_tensor(
                                out=dv, in0=pd[:, hf * 256:(hf + 1) * 256].rearrange("p (a q) -> p a q", a=2),
                                in1=sx, op=ALU.add), reads=[pdk, "sinkexp"], writes=[dk_])
                            S.op("dve", lambda e, dt_=dt_: e.reciprocal(out=dt_[:, 0:256], in_=dt_[:, 0:256]),
                                 reads=[dk_], writes=[dk_])
                            S.op("dve", lambda e, po=po, b=b, h0=h0, dv=dv, hf=hf: e.tensor_tensor(
                                out=oT[:, h0:h0 + 2, b * 128:(b + 1) * 128],
                                in0=po[:, hf * 256:(hf + 1) * 256].rearrange("p (a q) -> p a q", a=2),
                                in1=dv, op=ALU.mult), reads=[pok, dk_], writes=[("m_o", g)])

                def pool_unit(u):
                    if v == 1:
                        S.dma("sp", invc[:], dr["invcntc"][:, u, :], writes=["m_invc"])
                    else:
                        S.dma("sp", invc[:], dr["invcnt"][:, u, lt0:lt0 + TW], writes=["m_invc"])
                    w, wk = win_unit(POOL_OFF + u * 256)
                    wnd = POOL_WINDOWS[u]
                    for hh in range(2):
                        pc = 2 * u + hh
                        pu, puk = proj(w, wk, hh * 128, 0, XW)
                        ut, uk = tmp.get()
                        S.op("act", lambda e, ut=ut, pu=pu: e.copy(out=ut[:, :], in_=pu[:, 0:XW]), reads=[puk], writes=[uk])
                        cur, ck, step, ln = ut, uk, 1, XW
                        while step < wnd:
                            nt, nk_ = tmp.get()
                            ln2 = ln - step
                            S.op("pool", lambda e, nt=nt, cur=cur, step=step, ln2=ln2: e.tensor_tensor(
                                out=nt[:, 0:ln2], in0=cur[:, 0:ln2], in1=cur[:, step:step + ln2], op=ALU.add),
                                reads=[ck], writes=[nk_])
                            cur, ck, ln, step = nt, nk_, ln2, step * 2
                        st = MG - wnd // 2
                        mt, mk = tmp.get()
                        S.op("pool", lambda e, mt=mt, cur=cur, st=st, u=u: e.tensor_tensor(
                            out=mt[:, 0:TW], in0=cur[:, st:st + TW], in1=invc[:, :], op=ALU.mult),
                            reads=[ck, "m_invc"], writes=[mk])
                        S.op("pool", lambda e, mt=mt, ut=ut, pc=pc: e.tensor_tensor(
                            out=ypool[:, pc, :], in0=mt[:, 0:TW], in1=ut[:, MG:MG + TW], op=ALU.subtract),
                            reads=[mk, uk], writes=[("m_ypool", pc)])

                def poolw_all():
                    for gi in range(4):
                        pw, pwk = load_unit(U_PW + gi, 2)
                        for o_ in range(2):
                            ps, pk = self.PS.get()

                            def mmpw(e, ps=ps, gi=gi, o_=o_, pw=pw):
                                ins = None
                                for i in range(2):
                                    ins = e.matmul(ps[:, 0:TW], pw[:, i, o_ * 128:(o_ + 1) * 128], ypool[:, 2 * gi + i, :],
                                                   start=(i == 0), stop=(i == 1))
                                return ins
                            S.op("pe", mmpw, reads=[pwk, ("m_ypool", 2 * gi), ("m_ypool", 2 * gi + 1)], writes=[pk])
                            pcx = 2 * gi + o_
                            S.op("act", lambda e, ps=ps, pcx=pcx: e.activation(out=poolmix[:, pcx, :], in_=ps[:, 0:TW], func=AF.Identity,
                                                                              scale=self.pscales[:, l, pcx:pcx + 1]),
                                 reads=[pk, "pscales"], writes=[("m_poolmix", pcx)])

                def conv_unit(u):
                    wb, wbk = win_unit(CB_OFF + u * 256)
                    pbs = [proj(wb, wbk, hh * 128, MG, TW) for hh in range(2)]
                    bts = []
                    for hh in range(2):
                        bt, bk = tmp.get()
                        S.op("act", lambda e, bt=bt, p=pbs[hh][0]: e.copy(out=bt[:, 0:TW], in_=p[:, 0:TW]),
                             reads=[pbs[hh][1]], writes=[bk])
                        bts.append((bt, bk))
                    wc, wck = win_unit(CC_OFF + u * 256)
                    pcs = [proj(wc, wck, hh * 128, 0, XW) for hh in range(2)]
                    cts = []
                    for hh in range(2):
                        ct, ck = tmp.get()
                        S.op("act", lambda e, ct=ct, p=pcs[hh][0]: e.copy(out=ct[:, :], in_=p[:, 0:XW]),
                             reads=[pcs[hh][1]], writes=[ck])
                        cts.append((ct, ck))
                    wx, wxk = win_unit(CX_OFF + u * 256)
                    for hh in range(2):
                        j = 2 * u + hh
                        px, pxk = proj(wx, wxk, hh * 128, 0, XW)
                        ct, ck = cts[hh]
                        bt, bk = bts[hh]
                        S.op("dve", lambda e, ct=ct, px=px: e.tensor_tensor(out=ct[:, :], in0=px[:, 0:XW], in1=ct[:, :], op=ALU.mult),
                             reads=[pxk, ck], writes=[ck])
                        at, ak = tmp.get()
                        cw = self.convws
                        S.op("dve", lambda e, at=at, ct=ct, j=j: e.tensor_scalar_mul(
                            out=at[:, 0:TW], in0=ct[:, MG - 1:MG - 1 + TW], scalar1=cw[:, l, 0, j:j + 1]),
                            reads=[ck, "convws"], writes=[ak])
                        S.op("dve", lambda e, at=at, ct=ct, j=j: e.scalar_tensor_tensor(
                            out=at[:, 0:TW], in0=ct[:, MG:MG + TW], scalar=cw[:, l, 1, j:j + 1], in1=at[:, 0:TW],
                            op0=ALU.mult, op1=ALU.add), reads=[ck, ak, "convws"], writes=[ak])
                        S.op("dve", lambda e, at=at, ct=ct, j=j: e.scalar_tensor_tensor(
                            out=at[:, 0:TW], in0=ct[:, MG + 1:MG + 1 + TW], scalar=cw[:, l, 2, j:j + 1], in1=at[:, 0:TW],
                            op0=ALU.mult, op1=ALU.add), reads=[ck, ak, "convws"], writes=[ak])
                        S.op("pool", lambda e, at=at, bt=bt, j=j: e.tensor_tensor(
                            out=yconv[:, j, :], in0=at[:, 0:TW], in1=bt[:, 0:TW], op=ALU.mult),
                            reads=[ak, bk], writes=[("m_ypool", j)])

                pieces = [(lambda u=u: pool_unit(u)) for u in range(4)] + [poolw_all] + [(lambda u=u: conv_unit(u)) for u in range(4)]
                q_group(0)
                for g in range(NKV):
                    if g + 1 < NKV:
                        q_group(g + 1)
                    for b in range(2):
                        keys, nk = att_scores(g, b)
                        if pieces:
                            pieces.pop(0)()
                        att_pv(g, b, keys, nk)
                while pieces:
                    pieces.pop(0)()
                merged = bufQ
                allq = [("m_q", g) for g in range(NKV)]
                for dp in range(8):
                    sgs = [[None] * 3 for _ in range(2)]
                    for br in range(3):
                        wgu, wguk = win_unit(GATE_OFF + br * D + dp * 256)
                        for d_ in range(2):
                            pgt, pgk = proj(wgu, wguk, d_ * 128, MG, TW)
                            st_, sk_ = tmp.get()
                            S.op("act", lambda e, st_=st_, pgt=pgt: e.activation(out=st_[:, 0:TW], in_=pgt[:, 0:TW], func=AF.Sigmoid),
                                 reads=[pgk], writes=[sk_])
                            sgs[d_][br] = (st_, sk_)
                    srcs = [(U_AO, KC, oT, [("m_o", g) for g in range(NKV)]),
                            (U_PO, 8, poolmix, [("m_poolmix", i) for i in range(8)]),
                            (U_CO, 8, yconv, [("m_ypool", i) for i in range(8)])]
                    accs = [None, None]
                    for br, (wd, nkc, src, srck) in enumerate(srcs):
                        w, wk = load_unit(wd + dp, nkc)
                        for d_ in range(2):
                            ps, pk = self.PS.get()

                            def mmy(e, ps=ps, w=w, d_=d_, nkc=nkc, src=src):
                                ins = None
                                for k in range(nkc):
                                    ins = e.matmul(ps[:, 0:TW], w[:, k, d_ * 128:(d_ + 1) * 128], src[:, k, :],
                                                   start=(k == 0), stop=(k == nkc - 1))
                                return ins
                            S.op("pe", mmy, reads=[wk] + srck, writes=[pk])
                            st_, sk_ = sgs[d_][br]
                            dd = 2 * dp + d_
                            if br == 0:
                                S.op("dve", lambda e, st_=st_, ps=ps: e.tensor_tensor(
                                    out=st_[:, 0:TW], in0=ps[:, 0:TW], in1=st_[:, 0:TW], op=ALU.mult),
                                    reads=[pk, sk_], writes=[sk_])
                                accs[d_] = (st_, sk_)
                            else:
                                at, ak = accs[d_]
                                S.op("dve", lambda e, st_=st_, ps=ps: e.tensor_tensor(
                                    out=st_[:, 0:TW], in0=ps[:, 0:TW], in1=st_[:, 0:TW], op=ALU.mult),
                                    reads=[pk, sk_], writes=[sk_])
                                if br == 1:
                                    S.op("dve", lambda e, st_=st_, at=at: e.tensor_tensor(
                                        out=at[:, 0:TW], in0=at[:, 0:TW], in1=st_[:, 0:TW], op=ALU.add),
                                        reads=[ak, sk_], writes=[ak])
                                else:
                                    S.op("dve", lambda e, st_=st_, at=at, dd=dd: e.tensor_tensor(
                                        out=merged[:, dd, :], in0=at[:, 0:TW], in1=st_[:, 0:TW], op=ALU.add),
                                        reads=[ak, sk_] + allq, writes=[("m_mg", dd)] + ([("m_q", dd // 4)]))
                allmg = [("m_mg", i) for i in range(KC)]
                for dp in range(8):
                    w, wk = load_unit(U_WO + dp, KC)
                    for d_ in range(2):
                        dd = 2 * dp + d_
                        ps, pk = self.PS.get()

                        def mmo(e, ps=ps, w=w, d_=d_):
                            ins = None
                            for k in range(KC):
                                ins = e.matmul(ps[:, 0:TW], w[:, k, d_ * 128:(d_ + 1) * 128], merged[:, k, :],
                                               start=(k == 0), stop=(k == KC - 1))
                            return ins
                        S.op("pe", mmo, reads=[wk] + allmg, writes=[pk])
                        hr = hres(dd, col0, col0 + TW)
                        S.op("dve", lambda e, ps=ps, dd=dd, col0=col0, v=v: e.scalar_tensor_tensor(
                            out=self.h[:, dd, col0:col0 + TW], in0=ps[:, 0:TW], scalar=self.Gv[:, 1, v, dd:dd + 1],
                            in1=self.h[:, dd, col0:col0 + TW], op0=ALU.mult, op1=ALU.add),
                            reads=[pk, "Gv"] + hr, writes=hr)
            S.flush()

    def emit_final(self):
        S, nc, dr, h = self.S, self.nc, self.dram, self.h
        if not self.last:
            toks = []
            for c0 in range(0, KC, 4):
                toks.append(S.dma("sp", self.out[:, c0:c0 + 4, :], h[:, c0:c0 + 4, :],
                                  reads=[("h", c, b) for c in range(c0, c0 + 4) for b in range((HW + 127) // 128)]))
            S.wait_tokens("sp", toks)
            S.flush()
            return
        with ExitStack() as es:
            E = es.enter_context
            sqr = Ring([E(self.sbt(f"o_sq{i}", [128, 512], F32)) for i in range(2)], "o_sq")
            rstd = E(self.sbt("o_rstd", [128, 512], F32))
            ot = Ring([E(self.sbt(f"o_t{i}", [128, 4, 512], F32)) for i in range(2)], "o_t")
            toks = []
            for ti in range(2):
                col0 = LAT0 + 128 + ti * 512
                n = 512
                ps, pk = self.PS.get()
                for c in range(KC):
                    sq, sk = sqr.get()
                    S.op("act", lambda e, sq=sq, c=c, col0=col0: e.activation(out=sq[:, 0:n], in_=h[:, c, col0:col0 + n], func=AF.Square),
                         reads=hres(c, col0, col0 + n), writes=[sk])
                    S.op("pe", lambda e, sq=sq, c=c, ps=ps: e.matmul(ps[:, 0:n], self.ones32[:], sq[:, 0:n],
                                                                  start=(c == 0), stop=(c == KC - 1)),
                         reads=[sk, "ones32"], writes=[pk])
                S.op("act", lambda e, ps=ps: e.activation(out=rstd[:, :], in_=ps[:, :], func=AF.Sqrt,
                                                          bias=self.epst[:, 0:1], scale=1.0 / D),
                     reads=[pk, "eps"], writes=["o_rstd"])
                S.op("dve", lambda e: e.reciprocal(out=rstd[:, :], in_=rstd[:, :]), reads=["o_rstd"], writes=["o_rstd"])
                for c0 in range(0, KC, 4):
                    t, tk = ot.get()
                    for c in range(c0, c0 + 4):
                        S.op("dve", lambda e, t=t, c=c, c0=c0, col0=col0: e.scalar_tensor_tensor(
                            out=t[:, c - c0, :], in0=h[:, c, col0:col0 + n], scalar=self.fgs[:, c:c + 1], in1=rstd[:, :],
                            op0=ALU.mult, op1=ALU.mult), reads=hres(c, col0, col0 + n) + ["o_rstd", "fgs"], writes=[tk])
                    toks.append(S.dma("sp", self.out[:, c0:c0 + 4, ti * 512:(ti + 1) * 512], t[:, :, :], reads=[tk]))
            S.wait_tokens("sp", toks)
            S.flush()


def _fm(a):
    T = a.shape[0]
    return np.ascontiguousarray(a.reshape(T, KC, 128).transpose(2, 1, 0))


def _vec(a):
    sh = a.shape
    n = sh[-1] // 128
    a = a.reshape(sh[:-1] + (n, 128))
    return np.ascontiguousarray(np.moveaxis(a, -1, 0))


def _tables(core):
    lt = np.arange(NLT)
    gt = core * OWN - 256 + lt
    ok = (gt >= 0) & (gt < L)
    valid = np.broadcast_to(ok.astype(np.float32)[None, :], (128, NLT)).copy()
    gtc = np.clip(gt, 0, L - 1)
    row = (gtc // 64).astype(np.float32)
    colp = (gtc % 64).astype(np.float32)
    inv = (10000.0 ** (-np.arange(32, dtype=np.float32) / 32)).astype(np.float32)
    p = np.arange(128)
    pos = np.where((p < 64)[:, None], row[None, :], colp[None, :]).astype(np.float32)
    ang = pos * inv[p % 32][:, None]
    ang = ang.astype(np.float32)
    C = np.cos(ang).astype(np.float32)
    Sn = np.sin(ang).astype(np.float32)
    sign = np.where((p % 64) < 32, -1.0, 1.0).astype(np.float32)[:, None]
    Ssig = (Sn * sign).astype(np.float32)
    invcnt = np.zeros((128, 4, NLT), np.float32)
    for wi_, w in enumerate(POOL_WINDOWS):
        lo = np.clip(gt - w // 2, 0, L - 1)
        hi = np.clip(gt + (w - w // 2) - 1, 0, L - 1)
        cnt = np.maximum(hi - lo + 1, 1).astype(np.float32)
        invcnt[:, wi_, :] = (1.0 / cnt)[None, :]
    tc = np.arange(CTXL)
    invcntc = np.zeros((128, 4, CTXL), np.float32)
    for wi_, w in enumerate(POOL_WINDOWS):
        lo = np.clip(tc - w // 2, 0, CTXL - 1)
        hi = np.clip(tc + (w - w // 2) - 1, 0, CTXL - 1)
        invcntc[:, wi_, :] = (1.0 / (hi - lo + 1).astype(np.float32))[None, :]
    a = np.arange(128)[:, None]
    c = np.arange(128)[None, :]
    mask = np.zeros((128, 12, 2, 128), np.float32)
    for j in range(12):
        gb = core * 8 - 2 + j
        pv = (a >= c) & (0 <= gb - 1 < L // 128)
        nx = (a <= c) & (0 <= gb + 1 < L // 128)
        mask[:, j, 0, :] = np.where(pv, 0.0, MASKNEG)
        mask[:, j, 1, :] = np.where(nx, 0.0, MASKNEG)
    return dict(ropeC=C, ropeS=Ssig, valid=valid, invcnt=invcnt, invcntc=invcntc,
                maskb=mask.astype(ml_dtypes.bfloat16))


def _common_inputs(inp, layers):
    m = {}
    m["ccT"] = np.ascontiguousarray(np.stack([_vec(inp["c"][0]), _vec(inp["c_ctx"])], axis=-1))
    m["bada"] = _vec(inp["b_ada"])
    m["ng"] = _vec(inp["norm_g"])
    m["fg"] = _vec(inp["final_g"])
    m["sinkb"] = np.ascontiguousarray(np.broadcast_to(inp["attn_sink"][None], (128, 2, NQ)))
    m["pscale"] = _vec(inp["pool_scale"])
    m["convw"] = _vec(inp["conv_w"])
    m["ident"] = np.eye(128, dtype=np.float32).astype(ml_dtypes.bfloat16)
    for l in layers:
        for nm in ["w_ada", "ffn1_wi", "ffn1_wo", "w_in", "pool_w", "w_attn_out", "w_pool_out", "w_conv_out",
                   "w_o", "ffn2_wi", "ffn2_wo"]:
            m[f"{nm}{l}"] = inp[nm][l]
    return m


def _core_x(x, core):
    g0 = core * OWN - 256
    xe = np.zeros((NLT, D), np.float32)
    a, b = max(g0, 0), min(g0 + NLT, L)
    xe[a - g0:b - g0] = x[0, a:b]
    return _fm(xe)


_NC_CACHE = {}


def _get_nc(layers, first, last, stop=None):
    key = (tuple(layers), first, last, stop)
    if key not in _NC_CACHE:
        _NC_CACHE[key] = Builder(tuple(layers), first, last, stop).build()
    return _NC_CACHE[key]


def run_cores(inp, cores, layers=(0, 1), hin=None, stop=None):
    inp = {k: np.asarray(v, dtype=np.float32) for k, v in inp.items()}
    first = 0 in layers
    last = 1 in layers
    nc = _get_nc(layers, first, last, stop)
    common = _common_inputs(inp, layers)
    ctxT = _fm(inp["ctx"][0])
    in_maps = []
    for i, core in enumerate(cores):
        m = dict(common)
        m.update(_tables(core))
        if first:
            m["xin"] = _core_x(inp["x"], core)
            m["ctxin"] = ctxT
        else:
            m["hin"] = hin[i]
        in_maps.append(m)
    import os
    tr = bool(os.environ.get("K_TRACE"))
    res = run_bass_kernel_spmd(nc, in_maps, core_ids=list(range(len(cores))), trace=tr)
    if tr:
        print("exec_time_ns", res.exec_time_ns)
    return [r["out" if last else "hout"] for r in res.results]


def kernel(**inputs):
    outs = run_cores(inputs, list(range(NCORE)), layers=(0, 1))
    full = np.zeros((1, L, D), np.float32)
    for core, o in enumerate(outs):
        full[0, core * OWN:(core + 1) * OWN] = o.transpose(2, 1, 0).reshape(OWN, D)
    return full
```

```python
import math
from contextlib import ExitStack

import numpy as np
import ml_dtypes

import concourse.bass as bass
import concourse.mybir as mybir
from concourse.bass_utils import run_bass_kernel_spmd

F32 = mybir.dt.float32
BF16 = mybir.dt.bfloat16
AF = mybir.ActivationFunctionType
ALU = mybir.AluOpType

D = 2048
KC = 16
FF = 5632
L = 8192
NCORE = 8
OWN = 1024
CTXL = 256
NQ = 16
NKV = 4
Q_OFF = 0
K_OFF = 2048
V_OFF = 2560
POOL_OFF = 3072
CB_OFF = 4096
CC_OFF = 5120
CX_OFF = 6144
GATE_OFF = 7168
IN_COLS = 13312
POOL_WINDOWS = (2, 4, 8, 16)
EPS = 1e-6
NUNITS = 88
U_AO, U_PO, U_CO, U_WO, U_PW = 52, 60, 68, 76, 84
NLT = 1536
LAT0 = 8
NLATC = 1280
CTX0 = LAT0 + NLATC + 8
HW = CTX0 + CTXL + 8
MG = 8
TW = 256
XW = TW + 2 * MG
MASKNEG = -30000.0

ENGS = ["pe", "act", "dve", "pool", "sp"]
import os as _os
SAME_ENGINE_SYNC = not bool(_os.environ.get("K_NOSES"))


class Sched:
    def __init__(self, nc, eng_sems, dma_sems):
        self.nc = nc
        self.dma_sems = dma_sems
        self.dma_uses = {}
        self.dma_rr = {q: 0 for q in dma_sems}
        self.q = {e: [] for e in ENGS}
        self.cnt = {e: 0 for e in ENGS}
        self.lastw = {}
        self.readers = {}
        self.seen = {e: {} for e in ENGS}
        self.semobj = {}
        for e, s in eng_sems.items():
            self.semobj[("e", e)] = s
        for qn, lst in dma_sems.items():
            for i, s in enumerate(lst):
                self.semobj[("d", qn, i)] = s
                self.dma_uses[("d", qn, i)] = 0
        self.ninst = 0

    def _deps(self, eng, reads, writes):
        need = {}

        def add(k, v):
            if need.get(k, 0) < v:
                need[k] = v
        for r in reads:
            t = self.lastw.get(r)
            if t is not None:
                add(*t)
        for w in writes:
            t = self.lastw.get(w)
            if t is not None:
                add(*t)
            for k, v in self.readers.get(w, {}).items():
                add(k, v)
        waits = []
        for k, v in need.items():
            if k == ("e", eng) and not SAME_ENGINE_SYNC:
                continue
            if self.seen[eng].get(k, 0) >= v:
                continue
            self.seen[eng][k] = v
            waits.append((k, v))
        return waits

    def _commit(self, tok, reads, writes):
        k, v = tok
        for w in writes:
            self.lastw[w] = tok
            self.readers[w] = {}
        for r in reads:
            d = self.readers.setdefault(r, {})
            if d.get(k, 0) < v:
                d[k] = v

    def op(self, eng, fn, reads=(), writes=()):
        waits = self._deps(eng, reads, writes)
        self.cnt[eng] += 1
        tok = (("e", eng), self.cnt[eng])
        self.q[eng].append((waits, fn, ("e", eng), 1))
        self._commit(tok, reads, writes)
        return tok

    def dma(self, qeng, out, in_, reads=(), writes=()):
        ring = self.dma_sems[qeng]
        i = self.dma_rr[qeng]
        self.dma_rr[qeng] = (i + 1) % len(ring)
        k = ("d", qeng, i)
        prev = self.dma_uses[k] * 16
        self.dma_uses[k] += 1
        waits = self._deps(qeng, reads, writes)
        if prev > 0 and self.seen[qeng].get(k, 0) < prev:
            self.seen[qeng][k] = prev
            waits.append((k, prev))
        tok = (k, prev + 16)

        def fn(e, out=out, in_=in_):
            return e.dma_start(out=out, in_=in_)
        self.q[qeng].append((waits, fn, k, 16))
        self._commit(tok, reads, writes)
        return tok

    def wait_tokens(self, eng, toks):
        waits = []
        for k, v in toks:
            if self.seen[eng].get(k, 0) < v:
                self.seen[eng][k] = v
                waits.append((k, v))
        if waits:
            self.q[eng].append((waits, None, None, 0))

    def flush(self):
        nc = self.nc
        q = self.q
        self.q = {e: [] for e in ENGS}
        semobj = self.semobj
        for e in ENGS:
            self.ninst += len(q[e])

        def run(engobj, lst):
            for waits, fn, inck, incv in lst:
                for k, v in waits:
                    engobj.wait_ge(semobj[k], v)
                if fn is not None:
                    ins = fn(engobj)
                    ins.then_inc(semobj[inck], incv)

        with nc.Block() as block:
            if q["sp"]:
                @block.sync
                def _(e):
                    run(e, q["sp"])
            if q["pe"]:
                @block.tensor
                def _(e):
                    run(e, q["pe"])
            if q["act"]:
                @block.scalar
                def _(e):
                    run(e, q["act"])
            if q["dve"]:
                @block.vector
                def _(e):
                    run(e, q["dve"])
            if q["pool"]:
                @block.gpsimd
                def _(e):
                    run(e, q["pool"])


class Ring:
    def __init__(self, tiles, name, keys=None):
        self.tiles = tiles
        self.name = name
        self.keys = keys if keys is not None else [(name, i) for i in range(len(tiles))]
        self.i = 0

    def get(self):
        t = self.tiles[self.i]
        k = self.keys[self.i]
        self.i = (self.i + 1) % len(self.tiles)
        return t, k


def hres(c, a, b):
    return [("h", c, blk) for blk in range(a // 128, (b - 1) // 128 + 1)]


class Builder:
    def __init__(self, layers, first, last, stop=None):
        self.stop = stop
        self.layers = layers
        self.first = first
        self.last = last
        self.nc = bass.Bass("TRN2", target_bir_lowering=False)
        self.dram = {}

    def sbt(self, name, shape, dt=F32):
        self._uid = getattr(self, "_uid", 0) + 1
        return self.nc.sbuf_tensor(f"{name}_u{self._uid}", list(shape), dt)

    def din(self, name, shape, dt=F32):
        t = self.nc.dram_tensor(name, list(shape), dt, kind="ExternalInput")
        self.dram[name] = t
        return t

    def build(self):
        nc = self.nc
        dr = self.dram
        if self.first:
            self.din("xin", [128, KC, NLT])
            self.din("ctxin", [128, KC, CTXL])
        else:
            self.din("hin", [128, KC, HW])
        self.din("ccT", [128, KC, 2])
        self.din("bada", [128, 2, 144])
        self.din("ng", [128, 2, 3, KC])
        self.din("fg", [128, KC])
        self.din("sinkb", [128, 2, NQ])
        self.din("pscale", [128, 2, 8])
        self.din("convw", [128, 2, 3, 8])
        self.din("ropeC", [128, NLT])
        self.din("ropeS", [128, NLT])
        self.din("valid", [128, NLT])
        self.din("invcnt", [128, 4, NLT])
        self.din("invcntc", [128, 4, CTXL])
        self.din("maskb", [128, 12, 2, 128], BF16)
        self.din("ident", [128, 128], BF16)
        for l in self.layers:
            self.din(f"w_ada{l}", [D, 9 * D])
            self.din(f"ffn1_wi{l}", [D, 2 * FF])
            self.din(f"ffn1_wo{l}", [FF, D])
            self.din(f"w_in{l}", [D, IN_COLS])
            self.din(f"pool_w{l}", [4, 256, 256])
            self.din(f"w_attn_out{l}", [D, D])
            self.din(f"w_pool_out{l}", [1024, D])
            self.din(f"w_conv_out{l}", [1024, D])
            self.din(f"w_o{l}", [D, D])
            self.din(f"ffn2_wi{l}", [D, 2 * FF])
            self.din(f"ffn2_wo{l}", [FF, D])
        self.cache = {l: nc.dram_tensor(f"wcache{l}", [NUNITS, 128, KC * 256], BF16, kind="Internal") for l in self.layers}
        self.jobs = {l: self.make_jobs(l) for l in self.layers}
        if self.last:
            self.out = nc.dram_tensor("out", [128, KC, OWN], F32, kind="ExternalOutput")
        else:
            self.out = nc.dram_tensor("hout", [128, KC, HW], F32, kind="ExternalOutput")

        with ExitStack() as es:
            E = es.enter_context
            sb = lambda n, s, d=F32: E(nc.sbuf_tensor(n, list(s), d))
            self.h = sb("h_main", [128, KC, HW])
            self.Ko = sb("Ko", [128, NKV, 256], BF16)
            self.Vo = sb("Vo", [128, 2, 512], BF16)
            self.ones32 = sb("ones32", [128, 128])
            self.onesb = sb("onesb", [128, 128], BF16)
            self.identb = sb("identb", [128, 128], BF16)
            self.epst = sb("epst", [128, 1])
            self.mT = sb("mT", [128, 144, 2])
            self.Av = sb("Av", [128, 3, 2, KC])
            self.Bv = sb("Bv", [128, 3, 2, KC])
            self.Gv = sb("Gv", [128, 3, 2, KC])
            self.ngs = sb("ngs", [128, 2, 3, KC])
            self.fgs = sb("fgs", [128, KC])
            self.badas = sb("badas", [128, 2, 144])
            self.sinks = sb("sinks", [128, 2, NQ])
            self.sinkexp = sb("sinkexp", [128, NQ])
            self.pscales = sb("pscales", [128, 2, 8])
            self.convws = sb("convws", [128, 2, 3, 8])
            self.cc32 = sb("cc32", [128, KC, 2])
            self.scb = sb("scb", [128, KC, 2], BF16)
            self.psum = [E(nc.psum_tensor(f"ps{i}", [128, 512], F32)) for i in range(8)]
            self.PS = Ring(self.psum, "ps")
            eng_sems = {e: E(nc.semaphore("s_" + e)) for e in ENGS}
            dma_sems = {"sp": [E(nc.semaphore(f"d_sp{i}")) for i in range(8)],
                        "pool": [E(nc.semaphore(f"d_pool{i}")) for i in range(8)]}
            self.S = Sched(nc, eng_sems, dma_sems)
            self.out_toks = []

            self.emit_init()
            for l in self.layers:
                self.emit_ada(l)
                latg = [[(LAT0, 512, 0), (LAT0 + 512, 384, 0)], [(LAT0 + 896, 384, 0), (CTX0, 256, 1)]]
                if l == 0:
                    self.emit_pre()
                    self.emit_ffn(l, 1, self.h, latg, hres, jobs=(0, 2))
                    if self.stop == "ffn1":
                        break
                    kvt = [(LAT0 + 256 * i, 0, i * 2) for i in range(5)] + [(CTX0, 1, 10)]
                    qt = [(LAT0 + 256 * i, 0) for i in range(5)] + [(CTX0, 1)]
                    self.emit_mixer(l, kvt, qt)
                    if self.stop == "mixer":
                        break
                    self.emit_ffn(l, 2, self.h, latg, hres, jobs=(1, 2))
                else:
                    self.emit_ffn(l, 1, self.h, latg, hres)
                    kvt = [(LAT0 + 256 * i, 0, i * 2) for i in range(5)] + [(CTX0, 1, 10)]
                    qt = [(LAT0 + 128 + 256 * i, 0) for i in range(4)]
                    self.emit_mixer(l, kvt, qt)
                    o0 = LAT0 + 128
                    self.emit_ffn(l, 2, self.h, [[(o0, 512, 0)], [(o0 + 512, 512, 0)]], hres)
            self.emit_final()
        return nc

    def cunit(self, l, u, nk=KC):
        return self.cache[l][u, :, 0:nk * 256].rearrange("p (k c) -> p k c", c=256)

    def make_jobs(self, l):
        dr = self.dram
        jobs = []
        order = list(range(8, 12)) + list(range(0, 8)) + list(range(12, 52))
        for u in order:
            jobs.append((self.cunit(l, u), dr[f"w_in{l}"][:, u * 256:(u + 1) * 256].rearrange("(k p) c -> p k c", p=128), ("wc", l, u)))
        for gi in range(4):
            jobs.append((self.cunit(l, U_PW + gi, 2), dr[f"pool_w{l}"][gi, :, :].rearrange("(i p) c -> p i c", p=128), ("wc", l, U_PW + gi)))
        for base, nm, nk in ((U_AO, "w_attn_out", KC), (U_PO, "w_pool_out", 8), (U_CO, "w_conv_out", 8), (U_WO, "w_o", KC)):
            for dp in range(8):
                jobs.append((self.cunit(l, base + dp, nk),
                             dr[f"{nm}{l}"][:, dp * 256:(dp + 1) * 256].rearrange("(k p) c -> p k c", p=128), ("wc", l, base + dp)))
        return jobs

    def issue_jobs(self, l, n):
        jl = self.jobs.get(l)
        while jl and n > 0:
            dst, src, key = jl.pop(0)
            self.S.dma("pool", dst, src, writes=[key])
            n -= 1

    def emit_init(self):
        S, nc, dr, h = self.S, self.nc, self.dram, self.h
        allh = [("h", c, b) for c in range(KC) for b in range((HW + 127) // 128)]
        S.op("pool", lambda e: e.memset(self.ones32[:], 1.0), writes=["ones32"])
        S.op("pool", lambda e: e.memset(self.onesb[:], 1.0), writes=["onesb"])
        S.op("pool", lambda e: e.memset(self.epst[:], EPS), writes=["eps"])
        S.op("pool", lambda e: e.memset(h[:, :, 0:LAT0], 0.0), writes=allh)
        S.op("pool", lambda e: e.memset(h[:, :, LAT0 + NLATC:CTX0], 0.0), writes=allh)
        S.op("pool", lambda e: e.memset(h[:, :, CTX0 + CTXL:HW], 0.0), writes=allh)
        if self.first:
            for c0 in range(0, KC, 4):
                S.dma("sp", h[:, c0:c0 + 4, LAT0:LAT0 + NLATC], dr["xin"][:, c0:c0 + 4, 128:128 + NLATC], writes=allh)
            S.dma("sp", h[:, :, CTX0:CTX0 + CTXL], dr["ctxin"][:, :, :], writes=allh)
        else:
            for c0 in range(0, KC, 4):
                S.dma("sp", h[:, c0:c0 + 4, :], dr["hin"][:, c0:c0 + 4, :], writes=allh)
        S.dma("sp", self.identb[:], dr["ident"][:, :], writes=["identb"])
        S.dma("sp", self.ngs[:], dr["ng"][:, :, :, :], writes=["ngs"])
        S.dma("sp", self.fgs[:], dr["fg"][:, :], writes=["fgs"])
        S.dma("sp", self.badas[:], dr["bada"][:, :, :], writes=["badas"])
        S.dma("sp", self.sinks[:], dr["sinkb"][:, :, :], writes=["sinks"])
        S.dma("sp", self.pscales[:], dr["pscale"][:, :, :], writes=["pscales"])
        S.dma("sp", self.convws[:], dr["convw"][:, :, :, :], writes=["convws"])
        S.dma("sp", self.cc32[:], dr["ccT"][:, :, :], writes=["cc32"])
        S.op("act", lambda e: e.activation(out=self.scb[:], in_=self.cc32[:], func=AF.Silu),
             reads=["cc32"], writes=["scb"])
        S.flush()

    def emit_ada(self, l):
        S, nc, dr = self.S, self.nc, self.dram
        wada = dr[f"w_ada{l}"]
        with ExitStack() as es:
            wa = [es.enter_context(self.sbt(f"wa{l}_{i}", [128, KC, 512], BF16)) for i in range(3)]
            WA = Ring(wa, "wa")
            for piece in range(36):
                w, wk = WA.get()
                S.dma("pool", w[:], wada[:, piece * 512:(piece + 1) * 512].rearrange("(k p) c -> p k c", p=128),
                      writes=[wk])
                ps, pk = self.PS.get()

                def mm(e, w=w, ps=ps):
                    ins = None
                    for j in range(4):
                        for k in range(KC):
                            ins = e.matmul(ps[:, 2 * j:2 * j + 2], w[:, k, j * 128:(j + 1) * 128], self.scb[:, k, :],
                                           start=(k == 0), stop=(k == KC - 1))
                    return ins
                S.op("pe", mm, reads=[wk, "scb"], writes=[pk])
                psv = ps[:, 0:8].rearrange("p (j v) -> p j v", v=2)
                bb = self.badas[:, l, piece * 4:(piece + 1) * 4].unsqueeze(2).to_broadcast([128, 4, 2])
                S.op("dve", lambda e, psv=psv, bb=bb, piece=piece: e.tensor_tensor(
                    out=self.mT[:, piece * 4:(piece + 1) * 4, :], in0=psv, in1=bb, op=ALU.add),
                    reads=[pk, "badas"], writes=["mT"])
            for n in range(3):
                for v in range(2):
                    sh = self.mT[:, 16 * (3 * n):16 * (3 * n) + 16, v]
                    sc = self.mT[:, 16 * (3 * n + 1):16 * (3 * n + 1) + 16, v]
                    gt = self.mT[:, 16 * (3 * n + 2):16 * (3 * n + 2) + 16, v]
                    S.op("dve", lambda e, n=n, v=v, sc=sc: e.scalar_tensor_tensor(
                        out=self.Av[:, n, v, :], in0=sc, scalar=1.0, in1=self.ngs[:, l, n, :],
                        op0=ALU.add, op1=ALU.mult), reads=["mT", "ngs"], writes=["Av"])
                    S.op("dve", lambda e, n=n, v=v, sh=sh: e.tensor_copy(out=self.Bv[:, n, v, :], in_=sh),
                         reads=["mT"], writes=["Bv"])
                    gm = 1.0 if n == 1 else 0.5
                    S.op("dve", lambda e, n=n, v=v, gt=gt, gm=gm: e.tensor_scalar_mul(
                        out=self.Gv[:, n, v, :], in0=gt, scalar1=gm), reads=["mT"], writes=["Gv"])
            S.op("act", lambda e: e.activation(out=self.sinkexp[:], in_=self.sinks[:, l, :], func=AF.Exp),
                 reads=["sinks"], writes=["sinkexp"])
            S.flush()

    def norm_mod(self, hbuf, hresf, col0, n, nidx, v, dst, dstkey, tmp, sqr, rstd, rstdk, valid=None, validk=None):
        S = self.S
        ps, pk = self.PS.get()
        for c in range(KC):
            sq, sk = sqr.get()
            S.op("act", lambda e, sq=sq, c=c: e.activation(out=sq[:, 0:n], in_=hbuf[:, c, col0:col0 + n], func=AF.Square),
                 reads=hresf(c, col0, col0 + n), writes=[sk])
            S.op("pe", lambda e, sq=sq, c=c, ps=ps: e.matmul(ps[:, 0:n], self.ones32[:], sq[:, 0:n],
                                                          start=(c == 0), stop=(c == KC - 1)),
                 reads=[sk, "ones32"], writes=[pk])
        S.op("act", lambda e, ps=ps: e.activation(out=rstd[:, 0:n], in_=ps[:, 0:n], func=AF.Sqrt,
                                                  bias=self.epst[:, 0:1], scale=1.0 / D),
             reads=[pk, "eps"], writes=[rstdk])
        S.op("dve", lambda e: e.reciprocal(out=rstd[:, 0:n], in_=rstd[:, 0:n]), reads=[rstdk], writes=[rstdk])
        if valid is not None:
            S.op("dve", lambda e: e.tensor_tensor(out=rstd[:, 0:n], in0=rstd[:, 0:n], in1=valid, op=ALU.mult),
                 reads=[rstdk, validk], writes=[rstdk])
        for c in range(KC):
            t, tk = tmp.get()
            S.op("dve", lambda e, t=t, c=c: e.scalar_tensor_tensor(
                out=t[:, 0:n], in0=hbuf[:, c, col0:col0 + n], scalar=self.Av[:, nidx, v, c:c + 1], in1=rstd[:, 0:n],
                op0=ALU.mult, op1=ALU.mult), reads=hresf(c, col0, col0 + n) + [rstdk, "Av"], writes=[tk])
            if valid is None:
                S.op("act", lambda e, t=t, c=c: e.activation(out=dst(c), in_=t[:, 0:n], func=AF.Identity,
                                                            bias=self.Bv[:, nidx, v, c:c + 1], scale=1.0),
                     reads=[tk, "Bv"], writes=[dstkey])
            else:
                S.op("dve", lambda e, t=t, c=c: e.scalar_tensor_tensor(
                    out=dst(c), in0=valid, scalar=self.Bv[:, nidx, v, c:c + 1], in1=t[:, 0:n],
                    op0=ALU.mult, op1=ALU.add), reads=[tk, "Bv", validk], writes=[dstkey])

    def emit_ffn(self, l, which, hbuf, groups, hresf, jobs=None):
        S, nc, dr = self.S, self.nc, self.dram
        nidx = 0 if which == 1 else 2
        wi = dr[f"ffn{which}_wi{l}"]
        wo = dr[f"ffn{which}_wo{l}"]
        XNW = max(sum(t[1] for t in g) for g in groups)
        with ExitStack() as es:
            E = es.enter_context
            xn = E(self.sbt("f_xn", [128, KC, XNW], BF16))
            wig = [E(self.sbt(f"f_wig{i}", [128, KC, 256], BF16)) for i in range(2)]
            wiu = [E(self.sbt(f"f_wiu{i}", [128, KC, 256], BF16)) for i in range(2)]
            wos = [E(self.sbt(f"f_wo{i}", [128, 2, D], BF16)) for i in range(2)]
            h1 = Ring([E(self.sbt(f"f_h1{i}", [128, 512], BF16)) for i in range(4)], "f_h1")
            sg = Ring([E(self.sbt(f"f_sg{i}", [128, 512], F32)) for i in range(2)], "f_sg")
            tmp = Ring([E(self.sbt(f"f_tmp{i}", [128, 512], F32)) for i in range(2)], "f_tmp")
            sqr = Ring([E(self.sbt(f"f_sq{i}", [128, 512], F32)) for i in range(2)], "f_sq")
            rstds = [E(self.sbt(f"f_rstd{i}", [128, 512], F32)) for i in range(2)]
            ri = 0
            for gi, grp in enumerate(groups):
                offs = []
                o = 0
                for ti, (col0, n, v) in enumerate(grp):
                    offs.append(o)
                    rstd = rstds[ri % 2]
                    rk = ("f_rstd", ri % 2)
                    ri += 1
                    self.norm_mod(hbuf, hresf, col0, n, nidx, v,
                                  (lambda c, o=o, n=n: xn[:, c, o:o + n]), ("f_xn", ti), tmp, sqr, rstd, rk)
                    o += n
                for s in range(FF // 256):
                    sl = s % 2
                    S.dma("pool", wig[sl][:], wi[:, s * 256:(s + 1) * 256].rearrange("(k p) c -> p k c", p=128),
                          writes=[("f_wig", sl)])
                    S.dma("pool", wiu[sl][:], wi[:, FF + s * 256:FF + (s + 1) * 256].rearrange("(k p) c -> p k c", p=128),
                          writes=[("f_wiu", sl)])
                    S.dma("pool", wos[sl][:], wo[s * 256:(s + 1) * 256, :].rearrange("(j p) c -> p j c", p=128),
                          writes=[("f_wo", sl)])
                    if jobs is not None:
                        self.issue_jobs(jobs[0], jobs[1])
                    for ti, (col0, n, v) in enumerate(grp):
                        o = offs[ti]
                        hs = []
                        for j in range(2):
                            pg, pgk = self.PS.get()
                            pu, puk = self.PS.get()

                            def mmg(e, w=wig[sl], ps=pg, j=j, o=o, n=n):
                                ins = None
                                for k in range(KC):
                                    ins = e.matmul(ps[:, 0:n], w[:, k, j * 128:(j + 1) * 128], xn[:, k, o:o + n],
                                                   start=(k == 0), stop=(k == KC - 1))
                                return ins
                            S.op("pe", mmg, reads=[("f_wig", sl), ("f_xn", ti)], writes=[pgk])
                            S.op("pe", lambda e, w=wiu[sl], ps=pu, j=j, o=o, n=n: mmg(e, w, ps, j, o, n),
                                 reads=[("f_wiu", sl), ("f_xn", ti)], writes=[puk])
                            sgt, sgk = sg.get()
                            S.op("act", lambda e, sgt=sgt, pg=pg, n=n: e.activation(out=sgt[:, 0:n], in_=pg[:, 0:n], func=AF.Silu),
                                 reads=[pgk], writes=[sgk])
                            ht, hk = h1.get()
                            S.op("dve", lambda e, ht=ht, sgt=sgt, pu=pu, n=n: e.tensor_tensor(
                                out=ht[:, 0:n], in0=pu[:, 0:n], in1=sgt[:, 0:n], op=ALU.mult),
                                reads=[puk, sgk], writes=[hk])
                            hs.append((ht, hk))
                        for d in range(KC):
                            po, pok = self.PS.get()

                            def mmo(e, w=wos[sl], po=po, d=d, n=n, hs=hs):
                                ins = None
                                for j in range(2):
                                    ins = e.matmul(po[:, 0:n], w[:, j, d * 128:(d + 1) * 128], hs[j][0][:, 0:n],
                                                   start=(j == 0), stop=(j == 1))
                                return ins
                            S.op("pe", mmo, reads=[("f_wo", sl), hs[0][1], hs[1][1]], writes=[pok])
                            hr = hresf(d, col0, col0 + n)
                            S.op("dve", lambda e, po=po, d=d, n=n, col0=col0, v=v: e.scalar_tensor_tensor(
                                out=hbuf[:, d, col0:col0 + n], in0=po[:, 0:n], scalar=self.Gv[:, nidx, v, d:d + 1],
                                in1=hbuf[:, d, col0:col0 + n], op0=ALU.mult, op1=ALU.add),
                                reads=[pok, "Gv"] + hr, writes=hr)
                S.flush()

    def emit_pre(self):
        S, nc, dr = self.S, self.nc, self.dram
        with ExitStack() as es:
            ht = es.enter_context(self.sbt("h_tmp", [128, KC, 256], F32))
            hr = lambda c, a, b: [("ht", c)]
            allht = [("ht", c) for c in range(KC)]
            S.dma("sp", ht[:, :, 0:128], dr["xin"][:, :, 0:128], writes=allht)
            S.dma("sp", ht[:, :, 128:256], dr["xin"][:, :, NLT - 128:NLT], writes=allht)
            self.emit_ffn(0, 1, ht, [[(0, 256, 0)]], hr, jobs=(0, 1))
            for c0 in range(0, KC, 8):
                S.op("act", lambda e, c0=c0: e.copy(out=self.h[:, c0:c0 + 8, 0:LAT0], in_=ht[:, c0:c0 + 8, 128 - MG:128]),
                     reads=allht, writes=[("h", c, 0) for c in range(c0, c0 + 8)])
                S.op("act", lambda e, c0=c0: e.copy(out=self.h[:, c0:c0 + 8, LAT0 + NLATC:CTX0],
                                                    in_=ht[:, c0:c0 + 8, 128:128 + MG]),
                     reads=allht, writes=[("h", c, (LAT0 + NLATC) // 128) for c in range(c0, c0 + 8)])
            self.emit_mixer(0, [], [], pre=ht, pre_res=hr)

    def emit_mixer(self, l, kvtiles, qtiles, pre=None, pre_res=None):
        S, nc, dr = self.S, self.nc, self.dram
        w_in = dr[f"w_in{l}"]
        last_layer = (l == 1)
        scale = 1.0 / math.sqrt(128.0)
        with ExitStack() as es:
            E = es.enter_context
            sb = lambda n, s, d=F32: E(self.sbt(n, list(s), d))
            xn = sb("m_xn", [128, KC, XW], BF16)
            swb = [sb(f"m_sw{i}", [128, TW]) for i in range(1)]
            WU = Ring([sb(f"m_wu{i}", [128, KC, 256], BF16) for i in range(3)], "m_wu")
            tmp = Ring([sb(f"m_tmp{i}", [128, XW]) for i in range(6)], "m_tmp")
            sqr = tmp
            ropeCS = sb("m_ropeCS", [128, 2 * TW])
            ropeC = ropeCS[:, 0:TW]
            ropeS = ropeCS[:, TW:2 * TW]
            rstd = ropeCS
            validt = sb("m_valid", [128, XW])
            if pre is None:
                Kt = sb("m_K", [128, NKV, 12 * 128], BF16)
                Vt = sb("m_V", [128, 12, 512], BF16)
                bufQ = sb("m_q", [128, NQ, TW], BF16)
                oT = sb("m_o", [128, NQ, TW], BF16)
                ypool = sb("m_ypool", [128, 8, TW], BF16)
                poolmix = sb("m_poolmix", [128, 8, TW], BF16)
                yconv = ypool
                PT = sb("m_PT", [128, 5, 512], BF16)
                maskt = sb("m_mask", [128, 2, 2, 128], BF16)
                invc = sb("m_invc", [128, TW])
                snap = sb("m_snap", [128, KC, 4, MG])

            if pre is None:
                self.issue_jobs(l, 10 ** 6)

            def load_unit(u, nk=KC):
                w, wk = WU.get()
                S.dma("sp", w[:, 0:nk, :], self.cunit(l, u, nk), reads=[("wc", l, u)], writes=[wk])
                return w, wk

            def win_unit(c0):
                return load_unit(c0 // 256)

            def proj(w, wk, off, a, n, extra_reads=(), xb=None, xk="m_xn"):
                ps, pk = self.PS.get()
                xb = xn if xb is None else xb

                def mm(e):
                    ins = None
                    for k in range(KC):
                        ins = e.matmul(ps[:, 0:n], w[:, k, off:off + 128], xb[:, k, a:a + n],
                                       start=(k == 0), stop=(k == KC - 1))
                    return ins
                S.op("pe", mm, reads=[wk, xk] + list(extra_reads), writes=[pk])
                return ps, pk

            swi = [0]

            def rope_out(pt, ptk, n, dst, dstkey):
                i = swi[0] % len(swb)
                swi[0] += 1
                sw = swb[i]
                keys = [("m_sw", i, q) for q in range(4)]
                for q, (eng, dp, sp_) in enumerate((("act", 0, 32), ("act", 32, 0), ("dve", 64, 96), ("dve", 96, 64))):
                    if eng == "act":
                        S.op("act", lambda e, dp=dp, sp_=sp_: e.copy(out=sw[dp:dp + 32, 0:n], in_=pt[sp_:sp_ + 32, 0:n]),
                             reads=[ptk], writes=[keys[q]])
                    else:
                        S.op("dve", lambda e, dp=dp, sp_=sp_: e.tensor_copy(out=sw[dp:dp + 32, 0:n], in_=pt[sp_:sp_ + 32, 0:n]),
                             reads=[ptk], writes=[keys[q]])
                t1, t1k = tmp.get()
                S.op("dve", lambda e: e.tensor_tensor(out=t1[:, 0:n], in0=pt[:, 0:n], in1=ropeCS[:, 0:n], op=ALU.mult),
                     reads=[ptk, "m_rope"], writes=[t1k])
                S.op("pool", lambda e: e.tensor_tensor(out=sw[:, 0:n], in0=sw[:, 0:n], in1=ropeCS[:, TW:TW + n], op=ALU.mult),
                     reads=keys + ["m_rope"], writes=keys)
                S.op("pool", lambda e: e.tensor_tensor(out=dst, in0=t1[:, 0:n], in1=sw[:, 0:n], op=ALU.add),
                     reads=[t1k] + keys, writes=[dstkey])

            def do_norm(hbuf, hresf, e0, n, v, lt0, segs=None, xb=None, xk="m_xn"):
                if segs is None:
                    segs = [(hbuf, hresf, e0, n, 0)]
                xb = xn if xb is None else xb
                if v == 0:
                    S.dma("sp", validt[:, 0:n], dr["valid"][:, lt0:lt0 + n], writes=["m_valid"])
                for (sbuf_, sres, scol, sn, soff) in segs:
                    if v == 0:
                        self.norm_mod(sbuf_, sres, scol, sn, 1, v, (lambda c, soff=soff, sn=sn: xb[:, c, soff:soff + sn]),
                                      xk, tmp, sqr, rstd, "m_rope",
                                      valid=validt[:, soff:soff + sn], validk="m_valid")
                    else:
                        self.norm_mod(sbuf_, sres, scol, sn, 1, v, (lambda c, soff=soff, sn=sn: xb[:, c, soff:soff + sn]),
                                      xk, tmp, sqr, rstd, "m_rope")

            def kv_tile(hbuf, hresf, col0, n, v, lt0, Kdst, Kkey, Vdst, Vkey, xb=None, xk="m_xn"):
                xb = xn if xb is None else xb
                do_norm(hbuf, hresf, col0, n, v, lt0, xb=xb, xk=xk)
                if v == 0:
                    S.dma("sp", ropeCS[:, 0:n], dr["ropeC"][:, lt0:lt0 + n], writes=["m_rope"])
                    S.dma("sp", ropeCS[:, TW:TW + n], dr["ropeS"][:, lt0:lt0 + n], writes=["m_rope"])
                for u in range(2):
                    w, wk = win_unit(K_OFF + u * 256)
                    for hh in range(2):
                        g = 2 * u + hh
                        pt, ptk = proj(w, wk, hh * 128, 0, n, xb=xb, xk=xk)
                        if v == 0:
                            rope_out(pt, ptk, n, Kdst(g), Kkey)
                        else:
                            S.op("act", lambda e, pt=pt, g=g: e.copy(out=Kdst(g), in_=pt[:, 0:n]), reads=[ptk], writes=[Kkey])
                for u in range(2):
                    w, wk = win_unit(V_OFF + u * 256)
                    for hh in range(2):
                        g = 2 * u + hh
                        for b in range(n // 128):
                            ps, pk = self.PS.get()

                            def mmv(e, ps=ps, w=w, hh=hh, b=b):
                                ins = None
                                for k in range(KC):
                                    ins = e.matmul(ps[:, 0:128], xb[:, k, b * 128:(b + 1) * 128],
                                                   w[:, k, hh * 128:(hh + 1) * 128], start=(k == 0), stop=(k == KC - 1))
                                return ins
                            S.op("pe", mmv, reads=[wk, xk], writes=[pk])
                            S.op("act", lambda e, ps=ps, b=b, g=g: e.copy(out=Vdst(b, g), in_=ps[:, 0:128]),
                                 reads=[pk], writes=[Vkey])

            PS_full = self.PS
            if pre is None:
                self.PS = Ring(self.psum[0:6], "ps", [("ps", i) for i in range(6)])
                PSA = Ring(self.psum[6:8], "ps", [("ps", 6), ("ps", 7)])
            if pre is not None:
                for bi, lt0 in enumerate((0, NLT - 128)):
                    kv_tile(pre, pre_res, bi * 128, 128, 0, lt0,
                            (lambda g, bi=bi: self.Ko[:, g, bi * 128:(bi + 1) * 128]), "Ko",
                            (lambda b, g, bi=bi: self.Vo[:, bi, g * 128:(g + 1) * 128]), "Vo")
                S.flush()
                return

            allq0 = [("m_q", g) for g in range(NKV)]
            for ti0, (col0, v, kb0) in enumerate(kvtiles):
                lt0 = col0 + 120
                alt = (ti0 % 2 == 1)
                kv_tile(self.h, hres, col0, TW, v, lt0,
                        (lambda g, kb0=kb0: Kt[:, g, kb0 * 128:kb0 * 128 + TW]), ("m_K", kb0 // 2),
                        (lambda b, g, kb0=kb0: Vt[:, kb0 + b, g * 128:(g + 1) * 128]), ("m_V", kb0 // 2),
                        xb=(bufQ if alt else None), xk=("m_xnb" if alt else "m_xn"))
            if kvtiles:
                S.op("dve", lambda e: e.memset(bufQ[:, 0, 0:2], 0.0), reads=["m_xnb"], writes=allq0 + ["m_xnb"])
            allK = [("m_K", i) for i in range(6)] + ["Ko"]
            allV = [("m_V", i) for i in range(6)] + ["Vo"]

            def Kblk(g, ltb):
                if ltb == 0:
                    return self.Ko[:, g, 0:128]
                if ltb == 11:
                    return self.Ko[:, g, 128:256]
                i = ltb - 1 if ltb < 12 else ltb - 2
                return Kt[:, g, i * 128:(i + 1) * 128]

            def Vblk(g, ltb):
                if ltb == 0:
                    return self.Vo[:, 0, g * 128:(g + 1) * 128]
                if ltb == 11:
                    return self.Vo[:, 1, g * 128:(g + 1) * 128]
                i = ltb - 1 if ltb < 12 else ltb - 2
                return Vt[:, i, g * 128:(g + 1) * 128]


            snapres = lambda c, a, b: [("m_snap", a // MG)]
            need_snap = {}
            for ti, (col0, v) in enumerate(qtiles):
                if v == 0 and any(c2 + TW == col0 for (c2, v2) in qtiles[:ti]):
                    si = len(need_snap)
                    need_snap[ti] = si
                    S.op("act", lambda e, si=si, col0=col0: e.copy(out=snap[:, :, si, :], in_=self.h[:, :, col0 - MG:col0]),
                         reads=[("h", c, (col0 - MG) // 128) for c in range(KC)], writes=[("m_snap", si)])
            snapf = snap[:, :, :, :].rearrange("p k t m -> p k (t m)")
            for ti, (col0, v) in enumerate(qtiles):
                e0 = col0 - MG
                lt0 = col0 + 120
                if ti in need_snap:
                    do_norm(self.h, hres, e0, XW, v, lt0 - MG,
                            segs=[(snapf, snapres, need_snap[ti] * MG, MG, 0), (self.h, hres, col0, XW - MG, MG)])
                else:
                    do_norm(self.h, hres, e0, XW, v, lt0 - MG)
                if v == 1:
                    S.op("dve", lambda e: e.memset(xn[:, :, 0:MG], 0.0), reads=[], writes=["m_xn"])
                    S.op("dve", lambda e: e.memset(xn[:, :, MG + TW:XW], 0.0), reads=[], writes=["m_xn"])
                else:
                    S.dma("sp", ropeCS[:, 0:TW], dr["ropeC"][:, lt0:lt0 + TW], writes=["m_rope"])
                    S.dma("sp", ropeCS[:, TW:2 * TW], dr["ropeS"][:, lt0:lt0 + TW], writes=["m_rope"])
                    jb0 = lt0 // 128
                    S.dma("sp", maskt[:], dr["maskb"][:, jb0:jb0 + 2, :, :], writes=["m_mask"])
                def q_group(g):
                    for u in range(2):
                        w, wk = win_unit(Q_OFF + (4 * g + 2 * u) * 128)
                        for hh in range(2):
                            head = 4 * g + 2 * u + hh
                            pt, ptk = proj(w, wk, hh * 128, MG, TW)
                            if v == 0:
                                rope_out(pt, ptk, TW, bufQ[:, head, :], ("m_q", g))
                            else:
                                S.op("act", lambda e, pt=pt, head=head: e.copy(out=bufQ[:, head, :], in_=pt[:, 0:TW]),
                                     reads=[ptk], writes=[("m_q", g)])

                def att_scores(g, b):
                        if v == 0:
                            jb = lt0 // 128 + b
                            keys = [(jb - 1, 0), (jb, None), (jb + 1, 1), (12, None), (13, None)]
                        else:
                            keys = [(12, None), (13, None)]
                        qv = bufQ[:, 4 * g:4 * g + 4, b * 128:(b + 1) * 128]
                        nk = len(keys)
                        for ki, (kb, side) in enumerate(keys):
                            ps, pk = self.PS.get()

                            def mms(e, ps=ps, kb=kb, side=side, b=b, qv=qv, g=g):
                                pv = ps[:, :].rearrange("p (a q) -> p a q", a=4)
                                ins = e.matmul(pv, Kblk(g, kb), qv, start=True, stop=(side is None))
                                if side is not None:
                                    mb = maskt[:, b, side, :].unsqueeze(1).to_broadcast([128, 4, 128])
                                    ins = e.matmul(pv, self.identb[:], mb, start=False, stop=True)
                                return ins
                            S.op("pe", mms, reads=allK + [("m_q", g), "m_mask", "identb"], writes=[pk])
                            S.op("act", lambda e, ps=ps, ki=ki: e.activation(out=PT[:, ki, :], in_=ps[:, :], func=AF.Exp, scale=scale),
                                 reads=[pk], writes=[("m_PT", ki)])
                        return keys, nk

                def att_pv(g, b, keys, nk):
                        po, pok = PSA.get()
                        pd, pdk = PSA.get()

                        def mmpv(e, po=po, keys=keys, g=g):
                            ins = None
                            for ki, (kb, side) in enumerate(keys):
                                ins = e.matmul(po[:, :], Vblk(g, kb), PT[:, ki, :], start=(ki == 0), stop=(ki == len(keys) - 1))
                            return ins

                        def mmden(e, pd=pd, keys=keys):
                            ins = None
                            for ki in range(len(keys)):
                                ins = e.matmul(pd[:, :], self.onesb[:], PT[:, ki, :], start=(ki == 0), stop=(ki == len(keys) - 1))
                            return ins
                        ptr = [("m_PT", ki) for ki in range(nk)]
                        S.op("pe", mmpv, reads=allV + ptr, writes=[pok])
                        S.op("pe", mmden, reads=["onesb"] + ptr, writes=[pdk])
                        for hf in range(2):
                            dt_, dk_ = tmp.get()
                            h0 = 4 * g + 2 * hf
                            sx = self.sinkexp[:, h0:h0 + 2].unsqueeze(2).to_broadcast([128, 2, 128])
                            dv = dt_[:, 0:256].rearrange("p (a q) -> p a q", a=2)
                            S.op("dve", lambda e, pd=pd, sx=sx, dv=dv, hf=hf: e.tensor_tensor(
                                out=dv, in0=pd[:, hf * 256:(hf + 1) * 256].rearrange("p (a q) -> p a q", a=2),
                                in1=sx, op=ALU.add), reads=[pdk, "sinkexp"], writes=[dk_])
                            S.op("dve", lambda e, dt_=dt_: e.reciprocal(out=dt_[:, 0:256], in_=dt_[:, 0:256]),
                                 reads=[dk_], writes=[dk_])
                            S.op("dve", lambda e, po=po, b=b, h0=h0, dv=dv, hf=hf: e.tensor_tensor(
                                out=oT[:, h0:h0 + 2, b * 128:(b + 1) * 128],
                                in0=po[:, hf * 256:(hf + 1) * 256].rearrange("p (a q) -> p a q", a=2),
                                in1=dv, op=ALU.mult), reads=[pok, dk_], writes=[("m_o", g)])

                def pool_unit(u):
                    if v == 1:
                        S.dma("sp", invc[:], dr["invcntc"][:, u, :], writes=["m_invc"])
                    else:
                        S.dma("sp", invc[:], dr["invcnt"][:, u, lt0:lt0 + TW], writes=["m_invc"])
                    w, wk = win_unit(POOL_OFF + u * 256)
                    wnd = POOL_WINDOWS[u]
                    for hh in range(2):
                        pc = 2 * u + hh
                        pu, puk = proj(w, wk, hh * 128, 0, XW)
                        ut, uk = tmp.get()
                        S.op("act", lambda e, ut=ut, pu=pu: e.copy(out=ut[:, :], in_=pu[:, 0:XW]), reads=[puk], writes=[uk])
                        cur, ck, step, ln = ut, uk, 1, XW
                        while step < wnd:
                            nt, nk_ = tmp.get()
                            ln2 = ln - step
                            S.op("dve", lambda e, nt=nt, cur=cur, step=step, ln2=ln2: e.tensor_tensor(
                                out=nt[:, 0:ln2], in0=cur[:, 0:ln2], in1=cur[:, step:step + ln2], op=ALU.add),
                                reads=[ck], writes=[nk_])
                            cur, ck, ln, step = nt, nk_, ln2, step * 2
                        st = MG - wnd // 2
                        mt, mk = tmp.get()
                        S.op("dve", lambda e, mt=mt, cur=cur, st=st, u=u: e.tensor_tensor(
                            out=mt[:, 0:TW], in0=cur[:, st:st + TW], in1=invc[:, :], op=ALU.mult),
                            reads=[ck, "m_invc"], writes=[mk])
                        S.op("dve", lambda e, mt=mt, ut=ut, pc=pc: e.tensor_tensor(
                            out=ypool[:, pc, :], in0=mt[:, 0:TW], in1=ut[:, MG:MG + TW], op=ALU.subtract),
                            reads=[mk, uk], writes=[("m_ypool", pc)])

                def poolw_all():
                    for gi in range(4):
                        pw, pwk = load_unit(U_PW + gi, 2)
                        for o_ in range(2):
                            ps, pk = self.PS.get()

                            def mmpw(e, ps=ps, gi=gi, o_=o_, pw=pw):
                                ins = None
                                for i in range(2):
                                    ins = e.matmul(ps[:, 0:TW], pw[:, i, o_ * 128:(o_ + 1) * 128], ypool[:, 2 * gi + i, :],
                                                   start=(i == 0), stop=(i == 1))
                                return ins
                            S.op("pe", mmpw, reads=[pwk, ("m_ypool", 2 * gi), ("m_ypool", 2 * gi + 1)], writes=[pk])
                            pcx = 2 * gi + o_
                            S.op("act", lambda e, ps=ps, pcx=pcx: e.activation(out=poolmix[:, pcx, :], in_=ps[:, 0:TW], func=AF.Identity,
                                                                              scale=self.pscales[:, l, pcx:pcx + 1]),
                                 reads=[pk, "pscales"], writes=[("m_poolmix", pcx)])

                def conv_unit(u):
                    wb, wbk = win_unit(CB_OFF + u * 256)
                    pbs = [proj(wb, wbk, hh * 128, MG, TW) for hh in range(2)]
                    bts = []
                    for hh in range(2):
                        bt, bk = tmp.get()
                        S.op("act", lambda e, bt=bt, p=pbs[hh][0]: e.copy(out=bt[:, 0:TW], in_=p[:, 0:TW]),
                             reads=[pbs[hh][1]], writes=[bk])
                        bts.append((bt, bk))
                    wc, wck = win_unit(CC_OFF + u * 256)
                    pcs = [proj(wc, wck, hh * 128, 0, XW) for hh in range(2)]
                    cts = []
                    for hh in range(2):
                        ct, ck = tmp.get()
                        S.op("act", lambda e, ct=ct, p=pcs[hh][0]: e.copy(out=ct[:, :], in_=p[:, 0:XW]),
                             reads=[pcs[hh][1]], writes=[ck])
                        cts.append((ct, ck))
                    wx, wxk = win_unit(CX_OFF + u * 256)
                    for hh in range(2):
                        j = 2 * u + hh
                        px, pxk = proj(wx, wxk, hh * 128, 0, XW)
                        ct, ck = cts[hh]
                        bt, bk = bts[hh]
                        S.op("dve", lambda e, ct=ct, px=px: e.tensor_tensor(out=ct[:, :], in0=px[:, 0:XW], in1=ct[:, :], op=ALU.mult),
                             reads=[pxk, ck], writes=[ck])
                        at, ak = tmp.get()
                        cw = self.convws
                        S.op("dve", lambda e, at=at, ct=ct, j=j: e.tensor_scalar_mul(
                            out=at[:, 0:TW], in0=ct[:, MG - 1:MG - 1 + TW], scalar1=cw[:, l, 0, j:j + 1]),
                            reads=[ck, "convws"], writes=[ak])
                        S.op("dve", lambda e, at=at, ct=ct, j=j: e.scalar_tensor_tensor(
                            out=at[:, 0:TW], in0=ct[:, MG:MG + TW], scalar=cw[:, l, 1, j:j + 1], in1=at[:, 0:TW],
                            op0=ALU.mult, op1=ALU.add), reads=[ck, ak, "convws"], writes=[ak])
                        S.op("dve", lambda e, at=at, ct=ct, j=j: e.scalar_tensor_tensor(
                            out=at[:, 0:TW], in0=ct[:, MG + 1:MG + 1 + TW], scalar=cw[:, l, 2, j:j + 1], in1=at[:, 0:TW],
                            op0=ALU.mult, op1=ALU.add), reads=[ck, ak, "convws"], writes=[ak])
                        S.op("dve", lambda e, at=at, bt=bt, j=j: e.tensor_tensor(
                            out=yconv[:, j, :], in0=at[:, 0:TW], in1=bt[:, 0:TW], op=ALU.mult),
                            reads=[ak, bk], writes=[("m_ypool", j)])

                pieces = [(lambda u=u: pool_unit(u)) for u in range(4)] + [poolw_all] + [(lambda u=u: conv_unit(u)) for u in range(4)]
                q_group(0)
                for g in range(NKV):
                    if g + 1 < NKV:
                        q_group(g + 1)
                    for b in range(2):
                        keys, nk = att_scores(g, b)
                        if pieces:
                            pieces.pop(0)()
                        att_pv(g, b, keys, nk)
                while pieces:
                    pieces.pop(0)()
                merged = bufQ
                allq = [("m_q", g) for g in range(NKV)]
                for dp in range(8):
                    sgs = [[None] * 3 for _ in range(2)]
                    for br in range(3):
                        wgu, wguk = win_unit(GATE_OFF + br * D + dp * 256)
                        for d_ in range(2):
                            pgt, pgk = proj(wgu, wguk, d_ * 128, MG, TW)
                            st_, sk_ = tmp.get()
                            S.op("act", lambda e, st_=st_, pgt=pgt: e.activation(out=st_[:, 0:TW], in_=pgt[:, 0:TW], func=AF.Sigmoid),
                                 reads=[pgk], writes=[sk_])
                            sgs[d_][br] = (st_, sk_)
                    srcs = [(U_AO, KC, oT, [("m_o", g) for g in range(NKV)]),
                            (U_PO, 8, poolmix, [("m_poolmix", i) for i in range(8)]),
                            (U_CO, 8, yconv, [("m_ypool", i) for i in range(8)])]
                    accs = [None, None]
                    for br, (wd, nkc, src, srck) in enumerate(srcs):
                        w, wk = load_unit(wd + dp, nkc)
                        for d_ in range(2):
                            ps, pk = self.PS.get()

                            def mmy(e, ps=ps, w=w, d_=d_, nkc=nkc, src=src):
                                ins = None
                                for k in range(nkc):
                                    ins = e.matmul(ps[:, 0:TW], w[:, k, d_ * 128:(d_ + 1) * 128], src[:, k, :],
                                                   start=(k == 0), stop=(k == nkc - 1))
                                return ins
                            S.op("pe", mmy, reads=[wk] + srck, writes=[pk])
                            st_, sk_ = sgs[d_][br]
                            dd = 2 * dp + d_
                            if br == 0:
                                S.op("dve", lambda e, st_=st_, ps=ps: e.tensor_tensor(
                                    out=st_[:, 0:TW], in0=ps[:, 0:TW], in1=st_[:, 0:TW], op=ALU.mult),
                                    reads=[pk, sk_], writes=[sk_])
                                accs[d_] = (st_, sk_)
                            else:
                                at, ak = accs[d_]
                                S.op("dve", lambda e, st_=st_, ps=ps: e.tensor_tensor(
                                    out=st_[:, 0:TW], in0=ps[:, 0:TW], in1=st_[:, 0:TW], op=ALU.mult),
                                    reads=[pk, sk_], writes=[sk_])
                                if br == 1:
                                    S.op("dve", lambda e, st_=st_, at=at: e.tensor_tensor(
                                        out=at[:, 0:TW], in0=at[:, 0:TW], in1=st_[:, 0:TW], op=ALU.add),
                                        reads=[ak, sk_], writes=[ak])
                                else:
                                    S.op("dve", lambda e, st_=st_, at=at, dd=dd: e.tensor_tensor(
                                        out=merged[:, dd, :], in0=at[:, 0:TW], in1=st_[:, 0:TW], op=ALU.add),
                                        reads=[ak, sk_] + allq, writes=[("m_mg", dd)] + ([("m_q", dd // 4)]))
                allmg = [("m_mg", i) for i in range(KC)]
                for dp in range(8):
                    w, wk = load_unit(U_WO + dp, KC)
                    for d_ in range(2):
                        dd = 2 * dp + d_
                        ps, pk = self.PS.get()

                        def mmo(e, ps=ps, w=w, d_=d_):
                            ins = None
                            for k in range(KC):
                                ins = e.matmul(ps[:, 0:TW], w[:, k, d_ * 128:(d_ + 1) * 128], merged[:, k, :],
                                               start=(k == 0), stop=(k == KC - 1))
                            return ins
                        S.op("pe", mmo, reads=[wk] + allmg, writes=[pk])
                        hr = hres(dd, col0, col0 + TW)
                        S.op("dve", lambda e, ps=ps, dd=dd, col0=col0, v=v: e.scalar_tensor_tensor(
                            out=self.h[:, dd, col0:col0 + TW], in0=ps[:, 0:TW], scalar=self.Gv[:, 1, v, dd:dd + 1],
                            in1=self.h[:, dd, col0:col0 + TW], op0=ALU.mult, op1=ALU.add),
                            reads=[pk, "Gv"] + hr, writes=hr)
            S.flush()
            self.PS = PS_full

    def emit_final(self):
        S, nc, dr, h = self.S, self.nc, self.dram, self.h
        if not self.last:
            toks = []
            for c0 in range(0, KC, 4):
                toks.append(S.dma("sp", self.out[:, c0:c0 + 4, :], h[:, c0:c0 + 4, :],
                                  reads=[("h", c, b) for c in range(c0, c0 + 4) for b in range((HW + 127) // 128)]))
            S.wait_tokens("sp", toks)
            S.flush()
            return
        with ExitStack() as es:
            E = es.enter_context
            sqr = Ring([E(self.sbt(f"o_sq{i}", [128, 512], F32)) for i in range(2)], "o_sq")
            rstd = E(self.sbt("o_rstd", [128, 512], F32))
            ot = Ring([E(self.sbt(f"o_t{i}", [128, 4, 512], F32)) for i in range(2)], "o_t")
            toks = []
            for ti in range(2):
                col0 = LAT0 + 128 + ti * 512
                n = 512
                ps, pk = self.PS.get()
                for c in range(KC):
                    sq, sk = sqr.get()
                    S.op("act", lambda e, sq=sq, c=c, col0=col0: e.activation(out=sq[:, 0:n], in_=h[:, c, col0:col0 + n], func=AF.Square),
                         reads=hres(c, col0, col0 + n), writes=[sk])
                    S.op("pe", lambda e, sq=sq, c=c, ps=ps: e.matmul(ps[:, 0:n], self.ones32[:], sq[:, 0:n],
                                                                  start=(c == 0), stop=(c == KC - 1)),
                         reads=[sk, "ones32"], writes=[pk])
                S.op("act", lambda e, ps=ps: e.activation(out=rstd[:, :], in_=ps[:, :], func=AF.Sqrt,
                                                          bias=self.epst[:, 0:1], scale=1.0 / D),
                     reads=[pk, "eps"], writes=["o_rstd"])
                S.op("dve", lambda e: e.reciprocal(out=rstd[:, :], in_=rstd[:, :]), reads=["o_rstd"], writes=["o_rstd"])
                for c0 in range(0, KC, 4):
                    t, tk = ot.get()
                    for c in range(c0, c0 + 4):
                        S.op("dve", lambda e, t=t, c=c, c0=c0, col0=col0: e.scalar_tensor_tensor(
                            out=t[:, c - c0, :], in0=h[:, c, col0:col0 + n], scalar=self.fgs[:, c:c + 1], in1=rstd[:, :],
                            op0=ALU.mult, op1=ALU.mult), reads=hres(c, col0, col0 + n) + ["o_rstd", "fgs"], writes=[tk])
                    toks.append(S.dma("sp", self.out[:, c0:c0 + 4, ti * 512:(ti + 1) * 512], t[:, :, :], reads=[tk]))
            S.wait_tokens("sp", toks)
            S.flush()


def _fm(a):
    T = a.shape[0]
    return np.ascontiguousarray(a.reshape(T, KC, 128).transpose(2, 1, 0))


def _vec(a):
    sh = a.shape
    n = sh[-1] // 128
    a = a.reshape(sh[:-1] + (n, 128))
    return np.ascontiguousarray(np.moveaxis(a, -1, 0))


def _tables(core):
    lt = np.arange(NLT)
    gt = core * OWN - 256 + lt
    ok = (gt >= 0) & (gt < L)
    valid = np.broadcast_to(ok.astype(np.float32)[None, :], (128, NLT)).copy()
    gtc = np.clip(gt, 0, L - 1)
    row = (gtc // 64).astype(np.float32)
    colp = (gtc % 64).astype(np.float32)
    inv = (10000.0 ** (-np.arange(32, dtype=np.float32) / 32)).astype(np.float32)
    p = np.arange(128)
    pos = np.where((p < 64)[:, None], row[None, :], colp[None, :]).astype(np.float32)
    ang = pos * inv[p % 32][:, None]
    ang = ang.astype(np.float32)
    C = np.cos(ang).astype(np.float32)
    Sn = np.sin(ang).astype(np.float32)
    sign = np.where((p % 64) < 32, -1.0, 1.0).astype(np.float32)[:, None]
    Ssig = (Sn * sign).astype(np.float32)
    invcnt = np.zeros((128, 4, NLT), np.float32)
    for wi_, w in enumerate(POOL_WINDOWS):
        lo = np.clip(gt - w // 2, 0, L - 1)
        hi = np.clip(gt + (w - w // 2) - 1, 0, L - 1)
        cnt = np.maximum(hi - lo + 1, 1).astype(np.float32)
        invcnt[:, wi_, :] = (1.0 / cnt)[None, :]
    tc = np.arange(CTXL)
    invcntc = np.zeros((128, 4, CTXL), np.float32)
    for wi_, w in enumerate(POOL_WINDOWS):
        lo = np.clip(tc - w // 2, 0, CTXL - 1)
        hi = np.clip(tc + (w - w // 2) - 1, 0, CTXL - 1)
        invcntc[:, wi_, :] = (1.0 / (hi - lo + 1).astype(np.float32))[None, :]
    a = np.arange(128)[:, None]
    c = np.arange(128)[None, :]
    mask = np.zeros((128, 12, 2, 128), np.float32)
    for j in range(12):
        gb = core * 8 - 2 + j
        pv = (a >= c) & (0 <= gb - 1 < L // 128)
        nx = (a <= c) & (0 <= gb + 1 < L // 128)
        mask[:, j, 0, :] = np.where(pv, 0.0, MASKNEG)
        mask[:, j, 1, :] = np.where(nx, 0.0, MASKNEG)
    return dict(ropeC=C, ropeS=Ssig, valid=valid, invcnt=invcnt, invcntc=invcntc,
                maskb=mask.astype(ml_dtypes.bfloat16))


def _common_inputs(inp, layers):
    m = {}
    m["ccT"] = np.ascontiguousarray(np.stack([_vec(inp["c"][0]), _vec(inp["c_ctx"])], axis=-1))
    m["bada"] = _vec(inp["b_ada"])
    m["ng"] = _vec(inp["norm_g"])
    m["fg"] = _vec(inp["final_g"])
    m["sinkb"] = np.ascontiguousarray(np.broadcast_to(inp["attn_sink"][None], (128, 2, NQ)))
    m["pscale"] = _vec(inp["pool_scale"])
    m["convw"] = _vec(inp["conv_w"])
    m["ident"] = np.eye(128, dtype=np.float32).astype(ml_dtypes.bfloat16)
    for l in layers:
        for nm in ["w_ada", "ffn1_wi", "ffn1_wo", "w_in", "pool_w", "w_attn_out", "w_pool_out", "w_conv_out",
                   "w_o", "ffn2_wi", "ffn2_wo"]:
            m[f"{nm}{l}"] = inp[nm][l]
    return m


def _core_x(x, core):
    g0 = core * OWN - 256
    xe = np.zeros((NLT, D), np.float32)
    a, b = max(g0, 0), min(g0 + NLT, L)
    xe[a - g0:b - g0] = x[0, a:b]
    return _fm(xe)


_NC_CACHE = {}


def _get_nc(layers, first, last, stop=None):
    key = (tuple(layers), first, last, stop)
    if key not in _NC_CACHE:
        _NC_CACHE[key] = Builder(tuple(layers), first, last, stop).build()
    return _NC_CACHE[key]


def run_cores(inp, cores, layers=(0, 1), hin=None, stop=None):
    inp = {k: np.asarray(v, dtype=np.float32) for k, v in inp.items()}
    first = 0 in layers
    last = 1 in layers
    nc = _get_nc(layers, first, last, stop)
    common = _common_inputs(inp, layers)
    ctxT = _fm(inp["ctx"][0])
    in_maps = []
    for i, core in enumerate(cores):
        m = dict(common)
        m.update(_tables(core))
        if first:
            m["xin"] = _core_x(inp["x"], core)
            m["ctxin"] = ctxT
        else:
            m["hin"] = hin[i]
        in_maps.append(m)
    import os
    tr = bool(os.environ.get("K_TRACE"))
    res = run_bass_kernel_spmd(nc, in_maps, core_ids=list(range(len(cores))), trace=tr)
    if tr:
        print("exec_time_ns", res.exec_time_ns)
    return [r["out" if last else "hout"] for r in res.results]


def kernel(**inputs):
    outs = run_cores(inputs, list(range(NCORE)), layers=(0, 1))
    full = np.zeros((1, L, D), np.float32)
    for core, o in enumerate(outs):
        full[0, core * OWN:(core + 1) * OWN] = o.transpose(2, 1, 0).reshape(OWN, D)
    return full
```
